# Optimizing a Trainium2 kernel written in Bass

```python
import jax, jax.numpy as jnp
from jax import lax
import numpy as np

D_MODEL = 1024
BATCH = 4
SEQ = 8192
DEPTH = 1

CONV_DIM = 512
CONV_KERNEL = 31
HGRN_DIM = 1024
HGRN_HEADS = 8
HGRN_HEAD_DIM = HGRN_DIM // HGRN_HEADS
HGRN_CHUNK = 64
N_BRANCHES = 2
D_FF = 2816
FFN_KERNEL = 3
LN_EPS = 1e-5
RMS_EPS = 1e-6
ALPHA = (2.0 * DEPTH) ** 0.25
BETA = (8.0 * DEPTH) ** -0.25

IN_SPLITS = [CONV_DIM, CONV_DIM, HGRN_DIM, HGRN_DIM, HGRN_DIM, HGRN_DIM, N_BRANCHES * D_MODEL]
IN_COLS = sum(IN_SPLITS)
IN_OFFSETS = list(np.cumsum(IN_SPLITS)[:-1])

kernel_name = "hybrid_conformer_conv_hgrn2_gated_merge_convffn"


def layer_norm(x, g, b):
    xf = x.astype(jnp.float32)
    mu = jnp.mean(xf, axis=-1, keepdims=True)
    var = jnp.mean(jnp.square(xf - mu), axis=-1, keepdims=True)
    y = (xf - mu) * lax.rsqrt(var + LN_EPS) * g.astype(jnp.float32) + b.astype(jnp.float32)
    return y.astype(x.dtype)


def causal_dwconv(x, w, b):
    k_w = w.shape[0]
    c = x.shape[-1]
    y = lax.conv_general_dilated(
        x, w[:, None, :].astype(x.dtype), window_strides=(1,), padding=[(k_w - 1, 0)],
        dimension_numbers=("NWC", "WIO", "NWC"), feature_group_count=c)
    return y + b.astype(x.dtype)


def hgrn2_chunked(q, k, v, logf):
    bsz, seq, nh, dk = q.shape
    dv = v.shape[-1]
    nc = seq // HGRN_CHUNK

    def to_chunks(t):
        return t.reshape(bsz, nc, HGRN_CHUNK, nh, t.shape[-1]).transpose(1, 0, 3, 2, 4)

    qc, kc, vc, lfc = to_chunks(q), to_chunks(k), to_chunks(v), to_chunks(logf)
    bc = jnp.cumsum(lfc, axis=3)
    mask = jnp.tril(jnp.ones((HGRN_CHUNK, HGRN_CHUNK), dtype=bool))[:, :, None]

    def step(state, inp):
        q_c, k_c, v_c, b_c = inp
        diff = b_c[:, :, :, None, :] - b_c[:, :, None, :, :]
        decay = jnp.exp(jnp.where(mask, diff, -jnp.inf))
        scores = jnp.einsum("bhtk,bhsk,bhtsk->bhts", q_c, k_c, decay)
        o_intra = jnp.einsum("bhts,bhsv->bhtv", scores, v_c)
        o_inter = jnp.einsum("bhtk,bhkv->bhtv", q_c * jnp.exp(b_c), state)
        b_last = b_c[:, :, -1, :]
        k_tail = k_c * jnp.exp(b_last[:, :, None, :] - b_c)
        new_state = jnp.exp(b_last)[..., None] * state + jnp.einsum("bhsk,bhsv->bhkv", k_tail, v_c)
        return new_state, o_intra + o_inter

    s0 = jnp.zeros((bsz, nh, dk, dv), jnp.float32)
    _, oc = lax.scan(step, s0, (qc, kc, vc, bc))
    return oc.transpose(1, 0, 3, 2, 4).reshape(bsz, seq, nh, dv)


def setup_inputs(seed: int = 0) -> dict:
    key = jax.random.key(seed)
    ks = jax.random.split(key, 20)

    def nrm(k, shape, scale):
        return jax.random.normal(k, shape, jnp.float32) * scale

    col_scale = jnp.concatenate([
        jnp.ones((2 * CONV_DIM + 2 * HGRN_DIM,), jnp.float32),
        jnp.full((HGRN_DIM,), BETA, jnp.float32),
        jnp.ones((HGRN_DIM + N_BRANCHES * D_MODEL,), jnp.float32)])
    ffn_scale = jnp.concatenate([jnp.full((D_FF,), BETA, jnp.float32), jnp.ones((D_FF,), jnp.float32)])
    return {
        "x": nrm(ks[0], (BATCH, SEQ, D_MODEL), 1.0),
        "w_in": nrm(ks[1], (DEPTH, D_MODEL, IN_COLS), D_MODEL ** -0.5) * col_scale,
        "w_conv_dw": nrm(ks[2], (DEPTH, CONV_KERNEL, CONV_DIM), CONV_KERNEL ** -0.5),
        "b_conv_dw": nrm(ks[3], (DEPTH, CONV_DIM), 0.02),
        "conv_ln_g": 1.0 + nrm(ks[4], (DEPTH, CONV_DIM), 0.02),
        "conv_ln_b": nrm(ks[5], (DEPTH, CONV_DIM), 0.02),
        "w_conv_out": nrm(ks[6], (DEPTH, CONV_DIM, D_MODEL), BETA * CONV_DIM ** -0.5),
        "hgrn_lb_logits": nrm(ks[7], (DEPTH + 1, HGRN_DIM), 0.5),
        "hgrn_norm_g": 1.0 + nrm(ks[8], (DEPTH, HGRN_DIM), 0.02),
        "w_hgrn_out": nrm(ks[9], (DEPTH, HGRN_DIM, D_MODEL), BETA * HGRN_DIM ** -0.5),
        "w_out": nrm(ks[10], (DEPTH, D_MODEL, D_MODEL), BETA * D_MODEL ** -0.5),
        "ln1_g": 1.0 + nrm(ks[11], (DEPTH, D_MODEL), 0.02),
        "ln1_b": nrm(ks[12], (DEPTH, D_MODEL), 0.02),
        "w_ffn_in": nrm(ks[13], (DEPTH, D_MODEL, 2 * D_FF), D_MODEL ** -0.5) * ffn_scale,
        "w_ffn_dw": nrm(ks[14], (DEPTH, FFN_KERNEL, D_FF), FFN_KERNEL ** -0.5),
        "b_ffn_dw": nrm(ks[15], (DEPTH, D_FF), 0.02),
        "w_ffn_out": nrm(ks[16], (DEPTH, D_FF, D_MODEL), BETA * D_FF ** -0.5),
        "ln2_g": 1.0 + nrm(ks[17], (DEPTH, D_MODEL), 0.02),
        "ln2_b": nrm(ks[18], (DEPTH, D_MODEL), 0.02),
    }


def reference(x, w_in, w_conv_dw, b_conv_dw, conv_ln_g, conv_ln_b, w_conv_out,
              hgrn_lb_logits, hgrn_norm_g, w_hgrn_out, w_out, ln1_g, ln1_b,
              w_ffn_in, w_ffn_dw, b_ffn_dw, w_ffn_out, ln2_g, ln2_b):
    bsz, seq, _ = x.shape
    lb_all = jnp.cumsum(jax.nn.softmax(hgrn_lb_logits.astype(jnp.float32), axis=0), axis=0)
    for l in range(DEPTH):
        h = x
        proj = h @ w_in[l]
        c_val, c_gate, q_z, f_z, i_v, g_z, m_z = jnp.split(proj, IN_OFFSETS, axis=-1)

        c = c_val * jax.nn.sigmoid(c_gate)
        c = causal_dwconv(c, w_conv_dw[l], b_conv_dw[l])
        c = jax.nn.silu(layer_norm(c, conv_ln_g[l], conv_ln_b[l]))
        y_conv = c @ w_conv_out[l]

        lb = lb_all[l]
        zf = f_z.astype(jnp.float32)
        logf = jnp.log(lb + (1.0 - lb) * jax.nn.sigmoid(zf))
        k_in = (1.0 - lb) * jax.nn.sigmoid(-zf)
        qf = jax.nn.silu(q_z.astype(jnp.float32))
        heads = lambda t: t.reshape(bsz, seq, HGRN_HEADS, HGRN_HEAD_DIM)
        o = hgrn2_chunked(heads(qf), heads(k_in), heads(i_v.astype(jnp.float32)), heads(logf))
        o = o * lax.rsqrt(jnp.mean(jnp.square(o), axis=-1, keepdims=True) + RMS_EPS)
        o = o.reshape(bsz, seq, HGRN_DIM) * hgrn_norm_g[l].astype(jnp.float32)
        o = o.astype(x.dtype) * jax.nn.silu(g_z)
        y_hgrn = o @ w_hgrn_out[l]

        gates = jax.nn.sigmoid(m_z).reshape(bsz, seq, N_BRANCHES, D_MODEL)
        mixed = gates[:, :, 0, :] * y_conv + gates[:, :, 1, :] * y_hgrn
        mix = mixed @ w_out[l]
        x = layer_norm(ALPHA * x + mix, ln1_g[l], ln1_b[l])

        z = x @ w_ffn_in[l]
        u, gv = jnp.split(z, [D_FF], axis=-1)
        u = causal_dwconv(u, w_ffn_dw[l], b_ffn_dw[l])
        y_ffn = (jax.nn.gelu(u) * gv) @ w_ffn_out[l]
        x = layer_norm(ALPHA * x + y_ffn, ln2_g[l], ln2_b[l])
    return x
```

```python
import contextlib
import numpy as np
import concourse.bass as bass
import concourse.mybir as mybir
from concourse.bass_utils import run_bass_kernel_spmd

F32 = mybir.dt.float32
BF16 = mybir.dt.bfloat16
AF = mybir.ActivationFunctionType
ALU = mybir.AluOpType

D = 1024
SEQ = 8192
NB = 4
HALF = 4096
CONV_DIM = 512
CONV_K = 31
HG = 1024
D_FF = 2816
NFC = 22
LN_EPS = 1e-5
RMS_EPS = 1e-6
ALPHA = 2.0 ** 0.25
GC1 = 0.7978845608028654
GC2 = GC1 * 0.044715
UC = 2048

ENGS = ["pe", "act", "dve", "pool", "sp"]


class Buf:
    __slots__ = ("name", "w", "r")

    def __init__(self, name):
        self.name = name
        self.w = None
        self.r = []


class DmaSem:
    __slots__ = ("name", "count", "h")

    def __init__(self, name):
        self.name = name
        self.count = 0
        self.h = None


class Sched:
    def __init__(self):
        self.ops = {e: [] for e in ENGS}
        self.cnt = {e: 0 for e in ENGS}
        self.waited = {e: {} for e in ENGS}
        self.pending = {e: ([], []) for e in ENGS}
        self.dsems = []

    def dma_sem(self, name):
        s = DmaSem(name)
        self.dsems.append(s)
        return s

    def add(self, eng, fn, reads=(), writes=(), signal=True, dsem=None):
        deps = {}

        def need(ev):
            if ev is None:
                return
            k, v = ev
            if k == "pe" and eng == "pe":
                return
            if deps.get(k, 0) < v:
                deps[k] = v
        for b in reads:
            need(b.w)
        for b in writes:
            need(b.w)
            for ev in b.r:
                need(ev)
        waits = []
        for k, v in deps.items():
            if self.waited[eng].get(k, 0) >= v:
                continue
            self.waited[eng][k] = v
            waits.append((k, v))
        ev = None
        if dsem is not None:
            dsem.count += 16
            ev = (dsem, dsem.count)
            for b in reads:
                b.r.append(ev)
            for b in writes:
                b.w = ev
                b.r = []
        else:
            pr, pw = self.pending[eng]
            pr.extend(reads)
            pw.extend(writes)
            if signal:
                self.cnt[eng] += 1
                ev = (eng, self.cnt[eng])
                for b in pr:
                    b.r.append(ev)
                for b in pw:
                    b.w = ev
                    b.r = []
                self.pending[eng] = ([], [])
        self.ops[eng].append((waits, fn, ev))
        return ev

    def final_waits(self, eng, bufs):
        deps = {}
        for b in bufs:
            if b.w is not None:
                k, v = b.w
                deps[k] = max(deps.get(k, 0), v)
        self.ops[eng].append((list(deps.items()), None, None))

    def emit(self, nc):
        for e in ENGS:
            pr, pw = self.pending[e]
            assert not pr and not pw, f"pending unsignaled ops on {e}"
        with contextlib.ExitStack() as st:
            sems = {}
            for e in ENGS:
                sems[e] = st.enter_context(nc.semaphore(f"s_{e}"))
            for d in self.dsems:
                d.h = st.enter_context(nc.semaphore(f"d_{d.name}"))
            block = st.enter_context(nc.Block())

            def run(engobj, lst):
                for waits, fn, ev in lst:
                    for k, v in waits:
                        h = sems[k] if isinstance(k, str) else k.h
                        engobj.wait_ge(h, v)
                    if fn is None:
                        continue
                    ins = fn(engobj)
                    if ev is not None:
                        k, v = ev
                        if isinstance(k, str):
                            ins.then_inc(sems[k], 1)
                        else:
                            ins.then_inc(k.h, 16)

            @block.tensor
            def _(e):
                run(e, self.ops["pe"])

            @block.scalar
            def _(e):
                run(e, self.ops["act"])

            @block.vector
            def _(e):
                run(e, self.ops["dve"])

            @block.gpsimd
            def _(e):
                run(e, self.ops["pool"])

            @block.sync
            def _(e):
                run(e, self.ops["sp"])


IN_OFF = {"cv": 0, "cg": 512, "q": 1024, "f": 2048, "i": 3072, "g": 4096, "m0": 5120, "m1": 6144}


def unit_table():
    t = {}

    def add(name, mat, k_rows, col0, ncols):
        us = []
        for r0 in range(0, k_rows, 512):
            us.append((mat, r0, min(512, k_rows - r0), col0, ncols))
        t[name] = us
    add("cv", "w_in", D, 0, 512)
    add("cg", "w_in", D, 512, 512)
    for hg in range(2):
        for nm in ["f", "q", "i", "g"]:
            add(f"{nm}{hg}", "w_in", D, IN_OFF[nm] + hg * 512, 512)
        add(f"m0_{hg}", "w_in", D, IN_OFF["m0"] + hg * 512, 512)
        add(f"m1_{hg}", "w_in", D, IN_OFF["m1"] + hg * 512, 512)
        add(f"co{hg}", "w_conv_out", CONV_DIM, hg * 512, 512)
        add(f"ho{hg}", "w_hgrn_out", HG, hg * 512, 512)
        add(f"wo{hg}", "w_out", D, hg * 512, 512)
        add(f"fo{hg}", "w_ffn_out", D_FF, hg * 512, 512)
    for j in range(6):
        nc_ = 512 if j < 5 else 256
        add(f"u{j}", "w_ffn_in", D, j * 512, nc_)
        add(f"gv{j}", "w_ffn_in", D, D_FF + j * 512, nc_)
    return t


UT = unit_table()
UNAMES = list(UT.keys())
UBASE = {}
_n = 0
for _k in UNAMES:
    UBASE[_k] = _n
    _n += len(UT[_k])
NUNITS = _n

CO = {}
_c = 0
for _nm, _w in [("convw", 4 * CONV_K), ("convb", 4), ("clg", 4), ("clb", 4), ("l0", 8), ("l1", 8), ("gn", 8),
                ("ffw", NFC * 3), ("ffb", NFC), ("flag", 1), ("ident", 128), ("mask", 128)]:
    CO[_nm] = (_c, _w)
    _c += _w
NCONST = _c


def build_program():
    nc = bass.Bass("TRN2", target_bir_lowering=False)
    xc = nc.dram_tensor("xc", [2 * HALF, D], F32, kind="ExternalInput").ap()
    wpack = nc.dram_tensor("wpack", [NUNITS, 128, UC], F32, kind="ExternalInput").ap()
    cpack = nc.dram_tensor("cpack", [128, NCONST], F32, kind="ExternalInput").ap()
    lnpack = nc.dram_tensor("lnpack", [128, 4, D], F32, kind="ExternalInput").ap()
    outd = nc.dram_tensor("out", [HALF, D], F32, kind="ExternalOutput").ap()

    S = Sched()
    st = contextlib.ExitStack()
    with st:
        def sb(name, shape, dt):
            return st.enter_context(nc.sbuf_tensor(name, shape, dt))

        WMAX = 640
        NRING = 6
        ring = [sb(f"ring{i}", [128, 4, 512], BF16) for i in range(NRING)]
        ringb = [Buf(f"ring{i}") for i in range(NRING)]
        NSTG = 2
        stg = [sb(f"stg{i}", [128, 4, 512], F32) for i in range(NSTG)]
        stgb = [Buf(f"stg{i}") for i in range(NSTG)]
        stgsem = [S.dma_sem(f"stg{i}") for i in range(NSTG)]
        cst = sb("cst", [128, NCONST], F32)
        cstb = Buf("cst")
        lnt = sb("lnt", [128, 4, D], F32)
        lntb = Buf("lnt")
        xT = sb("xT", [128, 8, WMAX], BF16)
        xTb = Buf("xT")
        xtok = [sb(f"xtok{i}", [128, D], F32) for i in range(2)]
        xtokb = [Buf(f"xtok{i}") for i in range(2)]
        xtoksem = [S.dma_sem(f"xtok{i}") for i in range(2)]
        x1h = xtok[0]
        x1hb = xtokb[0]
        x1sem = [S.dma_sem(f"x1r{i}") for i in range(4)]
        cbuf = sb("cbuf", [128, 4, 32 + WMAX], BF16)
        cbufb = [Buf(f"cbuf{i}") for i in range(4)]
        cact = sb("cact", [128, 4, WMAX], BF16)
        cactb = Buf("cact")
        og = sb("og", [128, 8, WMAX], BF16)
        ogb = [Buf(f"og{i}") for i in range(8)]
        mixT = sb("mixT", [128, 8, WMAX], BF16)
        mixTb = Buf("mixT")
        x1tok = [sb(f"x1tok{i}", [128, D], F32) for i in range(4)]
        x1tokb = [Buf(f"x1tok{i}") for i in range(4)]
        outsem = [S.dma_sem(f"out{i}") for i in range(4)]
        outb = [Buf(f"outd{i}") for i in range(4)]
        hT = sb("hT", [128, NFC, 512], BF16)
        hTb = [Buf(f"hT{i}") for i in range(NFC)]
        vtok = sb("vtok", [128, 5, 512], BF16)
        vtokb = Buf("vtok")
        Sst = sb("Sst", [128, 8, 128], F32)
        Sbf = sb("Sbf", [128, 8, 128], BF16)
        Sstb = [Buf(f"S{i}") for i in range(8)]
        Sbfb = [Buf(f"Sb{i}") for i in range(8)]
        NT = 10
        T = [sb(f"T{i}", [128, WMAX], F32) for i in range(NT)]
        Tb = [Buf(f"T{i}") for i in range(NT)]
        Qb = [sb(f"Qb{i}", [128, WMAX], BF16) for i in range(4)]
        Kb = [sb(f"Kb{i}", [128, WMAX], BF16) for i in range(4)]
        Ktok = [sb(f"Ktok{i}", [128, WMAX], BF16) for i in range(4)]
        dch = [sb(f"dch{i}", [128, 16], F32) for i in range(4)]
        dmid = [sb(f"dmid{i}", [128, 16], F32) for i in range(4)]
        Qbb = [Buf(f"Qb{i}") for i in range(4)]
        Kbb = [Buf(f"Kb{i}") for i in range(4)]
        Ktokb = [Buf(f"Ktok{i}") for i in range(4)]
        dchb = [Buf(f"dch{i}") for i in range(4)]
        osq = sb("osq", [128, WMAX], BF16)
        osqb = Buf("osq")
        smk = [sb(f"smk{i}", [128, 128], BF16) for i in range(2)]
        smkb = [Buf(f"smk{i}") for i in range(2)]
        NDG = 8
        dg = [sb(f"dg{i}", [128, 128], BF16) for i in range(NDG)]
        dgb = [Buf(f"dg{i}") for i in range(NDG)]
        ubuf = [sb(f"ubuf{i}", [128, 2 + WMAX], BF16) for i in range(2)]
        ubufb = [Buf(f"ubuf{i}") for i in range(2)]
        uhist = sb("uhist", [128, NFC, 2], BF16)
        uhistb = Buf("uhist")
        dcon = sb("dcon", [128, 64], F32)
        dconb = Buf("dcon")
        onesC = sb("onesC", [128, 128], BF16)
        onesH = sb("onesH", [128, 128], BF16)
        identb = sb("identb", [128, 128], BF16)
        maskb = sb("maskb", [128, 128], BF16)
        negh = sb("negh", [128, WMAX], F32)
        miscb = Buf("misc")
        bst = sb("bst", [128, 2, 6], F32)
        mv = sb("mv", [128, 2], F32)
        bstb = Buf("bst")
        mvb = Buf("mv")
        psum = [st.enter_context(nc.psum_tensor(f"ps{i}", [128, 512], F32)) for i in range(8)]
        psb = [Buf(f"ps{i}") for i in range(8)]

        state = {"ps": 0, "ring": 0, "stg": 0, "dg": 0, "xtok": 0, "smk": 0, "alt": 0}

        held = set()

        def next_ps(hold=False):
            for _ in range(8):
                i = state["ps"]
                state["ps"] = (i + 1) % 8
                if i not in held:
                    if hold:
                        held.add(i)
                    return psum[i], psb[i]
            raise RuntimeError("no free psum bank")

        def unhold(pb):
            held.discard(psb.index(pb))

        def cs(name, j=0, n=1):
            o, w = CO[name]
            return cst[:, o + j:o + j + n]

        DC = {"a0": 0, "na0": 8, "c0": 16, "gnh": 24, "clgh": 32, "clbh": 36, "eps_ln": 40, "eps_rms": 41, "negh": 42}

        def dc(name, j=0):
            o = DC[name]
            return dcon[:, o + j:o + j + 1]

        def act(out, in_, func, reads, writes, scale=None, bias=None):
            kw = {}
            if scale is not None:
                kw["scale"] = scale
            if bias is not None:
                kw["bias"] = bias
            S.add("act", lambda e: e.activation(out=out, in_=in_, func=func, **kw), reads=reads, writes=writes)

        def tt(eng, out, in0, in1, op, reads, writes):
            S.add(eng, lambda e: e.tensor_tensor(out=out, in0=in0, in1=in1, op=op), reads=reads, writes=writes)

        def ts(eng, out, in0, s1, s2, op0, op1, reads, writes):
            if op1 is None:
                S.add(eng, lambda e: e.tensor_scalar(out=out, in0=in0, scalar1=s1, scalar2=None, op0=op0), reads=reads, writes=writes)
            else:
                S.add(eng, lambda e: e.tensor_scalar(out=out, in0=in0, scalar1=s1, scalar2=s2, op0=op0, op1=op1), reads=reads, writes=writes)

        def stt(out, in0, scalar, in1, op0, op1, reads, writes):
            S.add("dve", lambda e: e.scalar_tensor_tensor(out=out, in0=in0, scalar=scalar, in1=in1, op0=op0, op1=op1), reads=reads, writes=writes)

        def mm(out, lhsT, rhs, start, stop, reads, writes, signal):
            S.add("pe", lambda e: e.matmul(out, lhsT=lhsT, rhs=rhs, start=start, stop=stop), reads=reads, writes=writes, signal=signal)

        def cp(eng, out, in_, reads, writes):
            if eng == "act":
                act(out, in_, AF.Copy, reads, writes)
            else:
                S.add(eng, lambda e: e.tensor_copy(out=out, in_=in_), reads=reads, writes=writes)

        def alt2():
            state["alt"] ^= 1
            return "act" if state["alt"] else "dve"

        csem = S.dma_sem("consts")
        S.add("sp", lambda e: e.dma_start(out=cst[:], in_=cpack), writes=[cstb], dsem=csem)
        S.add("sp", lambda e: e.dma_start(out=lnt[:], in_=lnpack), writes=[lntb], dsem=csem)
        S.add("pool", lambda e: e.memset(onesC[:], 1.0 / 512.0), writes=[miscb])
        S.add("pool", lambda e: e.memset(onesH[:], 1.0 / 128.0), writes=[miscb])
        S.add("pool", lambda e: e.memset(negh[:], -0.5), writes=[miscb])
        S.add("pool", lambda e: e.memset(Sst[:], 0.0), writes=Sstb)
        S.add("pool", lambda e: e.memset(Sbf[:], 0.0), writes=Sbfb)
        S.add("pool", lambda e: e.memset(uhist[:], 0.0), writes=[uhistb])
        S.add("pool", lambda e: e.memset(cbuf[:], 0.0), writes=cbufb)
        S.add("pool", lambda e: e.memset(dcon[:], 0.0), writes=[dconb])
        cp("dve", identb[:], cs("ident", 0, 128), [cstb], [miscb])
        cp("dve", maskb[:], cs("mask", 0, 128), [cstb], [miscb])
        tt("dve", dcon[:, 48:56], cs("l0", 0, 8), cs("l1", 0, 8), ALU.subtract, [cstb, dconb], [dconb])
        act(dcon[:, 48:56], dcon[:, 48:56], AF.Tanh, [dconb], [dconb], scale=0.5)
        ts("dve", dcon[:, 48:56], dcon[:, 48:56], 0.5, 0.5, ALU.mult, ALU.add, [dconb], [dconb])
        ts("dve", dcon[:, 0:8], dcon[:, 48:56], -0.5, 0.5, ALU.mult, ALU.add, [dconb], [dconb])
        ts("dve", dcon[:, 8:16], dcon[:, 48:56], 0.5, -0.5, ALU.mult, ALU.add, [dconb], [dconb])
        ts("dve", dcon[:, 16:24], dcon[:, 48:56], 0.5, 0.5, ALU.mult, ALU.add, [dconb], [dconb])
        ts("dve", dcon[:, 24:32], cs("gn", 0, 8), 0.5, None, ALU.mult, None, [cstb, dconb], [dconb])
        ts("dve", dcon[:, 32:36], cs("clg", 0, 4), 0.5, None, ALU.mult, None, [cstb, dconb], [dconb])
        ts("dve", dcon[:, 36:40], cs("clb", 0, 4), 0.5, None, ALU.mult, None, [cstb, dconb], [dconb])
        S.add("pool", lambda e: e.memset(dcon[:, 40:41], LN_EPS), reads=[dconb], writes=[dconb])
        S.add("pool", lambda e: e.memset(dcon[:, 41:42], RMS_EPS), reads=[dconb], writes=[dconb])
        S.add("pool", lambda e: e.memset(dcon[:, 42:43], -0.5), reads=[dconb], writes=[dconb])

        wq = []
        loaded = {}

        class WStream:
            def __init__(self):
                self.seq = []
                self.next_load = 0
                self.slot_of = {}

            def plan(self, units):
                self.seq.extend(units)

            def ensure(self, upto):
                while self.next_load < min(upto, len(self.seq)):
                    pos = self.next_load
                    u = self.seq[pos]
                    si = state["stg"]
                    state["stg"] = (si + 1) % NSTG
                    ri = state["ring"]
                    state["ring"] = (ri + 1) % NRING
                    S.add("sp", lambda e, u=u, si=si: e.dma_start(out=stg[si][:].rearrange("p a b -> p (a b)"), in_=wpack[u]),
                          writes=[stgb[si]], dsem=stgsem[si])
                    S.add("pool", lambda e, si=si, ri=ri: e.tensor_copy(out=ring[ri][:], in_=stg[si][:]),
                          reads=[stgb[si]], writes=[ringb[ri]])
                    self.slot_of[pos] = ri
                    self.next_load += 1

        WS = WStream()
        wpos = {"p": 0, "live": 0}
        PREFETCH = 3

        def rel():
            wpos["live"] = wpos["p"]
            WS.ensure(wpos["p"] + PREFETCH)

        def take(name):
            n = len(UT[name])
            res = []
            for i in range(n):
                pos = wpos["p"]
                assert WS.seq[pos] == UBASE[name] + i, (name, i, pos)
                assert pos + 1 - wpos["live"] <= NRING, (name, pos, wpos["live"])
                WS.ensure(pos + 1)
                ri = WS.slot_of[pos]
                res.append((ring[ri], ringb[ri]))
                wpos["p"] += 1
            return res

        def wsl(units, k, c0, n):
            r, b = units[k // 4]
            return r[:, k % 4, c0:c0 + n], b

        def main_units():
            names = ["cv", "cg"]
            for hg in range(2):
                names += [f"f{hg}", f"q{hg}", f"i{hg}", f"g{hg}"]
            for hg in range(2):
                names += [f"m0_{hg}", f"co{hg}", f"m1_{hg}", f"ho{hg}"]
            names += ["wo0", "wo1"]
            for j in range(6):
                names += [f"u{j}", f"gv{j}"]
            names += ["fo0", "fo1"]
            return names

        def pre_units(last):
            names = []
            if last:
                names += ["cv", "cg"]
            for hg in range(2):
                names += [f"f{hg}", f"i{hg}"]
            return names

        tiles = []
        r0 = 0
        while r0 < HALF - 128:
            w = min(512, HALF - 128 - r0)
            tiles.append(("pre", r0, w, r0 + w >= HALF - 128))
            r0 += w
        tiles.append(("main", HALF - 128, 640, True))
        for i in range(1, 8):
            tiles.append(("main", HALF + 512 * i, 512, False))
        for kind, row0, W, flag in tiles:
            names = pre_units(flag) if kind == "pre" else main_units()
            for nm in names:
                WS.plan([UBASE[nm] + i for i in range(len(UT[nm]))])

        def groups_of(W):
            return [(0, 128), (128, 512)] if W == 640 else [(0, W)]

        def load_x(row0, W):
            for p in range(W // 128):
                xi = state["xtok"]
                state["xtok"] ^= 1
                S.add("pool", lambda e, xi=xi, r=row0 + p * 128: e.dma_start(out=xtok[xi][:], in_=xc[r:r + 128, :]),
                      writes=[xtokb[xi]], dsem=xtoksem[xi])
                for kb in range(2):
                    ps, pb = next_ps()
                    for kk in range(4):
                        k = kb * 4 + kk
                        S.add("pe", lambda e, ps=ps, kk=kk, k=k, xi=xi: e.transpose(out=ps[:, kk * 128:(kk + 1) * 128], in_=xtok[xi][:, k * 128:(k + 1) * 128], identity=cs("ident", 0, 128)),
                              reads=[xtokb[xi], cstb], writes=[pb], signal=(kk == 3))
                    cp(alt2(), xT[:, kb * 4:kb * 4 + 4, p * 128:(p + 1) * 128], ps[:, :].rearrange("p (k c) -> p k c", c=128), [pb], [xTb])

        def proj_fm(units, cl, g, kn=8, src=None, srcb=None):
            c0, n = g
            ps, pb = next_ps()
            src = xT if src is None else src
            srcb = [xTb] if srcb is None else srcb
            for k in range(kn):
                l, lb_ = wsl(units, k, cl * 128, 128)
                mm(ps[:, 0:n], l, src[:, k, c0:c0 + n], k == 0, k == kn - 1, [lb_] + srcb, [pb], k == kn - 1)
            return ps, pb

        def glu_stage(W):
            ucv = take("cv")
            ucg = take("cg")
            for ch in range(4):
                for g in groups_of(W):
                    c0, n = g
                    pv, pvb = proj_fm(ucv, ch, g)
                    pg, pgb = proj_fm(ucg, ch, g)
                    act(T[0][:, c0:c0 + n], pg[:, 0:n], AF.Tanh, [pgb], [Tb[0]], scale=0.5)
                    stt(cbuf[:, ch, 32 + c0:32 + c0 + n], T[0][:, c0:c0 + n], 1.0, pv[:, 0:n], ALU.add, ALU.mult, [Tb[0], pvb], [cbufb[ch]])
            rel()

        def cbuf_carry(W):
            for ch in range(4):
                cp("pool", cbuf[:, ch, 0:32], cbuf[:, ch, W:W + 32], [cbufb[ch]], [cbufb[ch]])

        def conv_stage(W):
            gs = groups_of(W)
            for ch in range(4):
                pss = [next_ps(hold=True) for _ in gs]
                for k in range(CONV_K):
                    tj, ts_ = k // 10, (k % 10) * 128
                    dgap = T[tj][:].bitcast(BF16)[:, ts_:ts_ + 128]
                    ts("pool", dgap, cs("ident", 0, 128), cs("convw", ch * CONV_K + k, 1), 0.5, ALU.mult, ALU.mult, [cstb], [Tb[tj]])
                    for gi_, ((c0, n), (ps, pb)) in enumerate(zip(gs, pss)):
                        mm(ps[:, 0:n], dgap, cbuf[:, ch, c0 + 2 + k:c0 + 2 + k + n], k == 0, k == CONV_K - 1,
                           [Tb[tj], cbufb[ch]], [pb], k == CONV_K - 1)
                for (c0, n), (ps, pb) in zip(gs, pss):
                    act(T[5 + ch][:, c0:c0 + n], ps[:, 0:n], AF.Identity, [pb, cstb], [Tb[5 + ch]], bias=cs("convb", ch, 1))
                    act(og[:, 4 + ch, c0:c0 + n], ps[:, 0:n], AF.Square, [pb, cstb], [ogb[4 + ch]], bias=cs("convb", ch, 1))
                    cp("dve", og[:, ch, c0:c0 + n], T[5 + ch][:, c0:c0 + n], [Tb[5 + ch]], [ogb[ch]])
                    unhold(pb)
            for (c0, n) in gs:
                pm, pmb = next_ps()
                for ch in range(4):
                    mm(pm[:, 0:n], onesC[:], og[:, ch, c0:c0 + n], ch == 0, ch == 3, [miscb, ogb[ch]], [pmb], ch == 3)
                pq, pqb = next_ps()
                for ch in range(4):
                    mm(pq[:, 0:n], onesC[:], og[:, 4 + ch, c0:c0 + n], ch == 0, ch == 3, [miscb, ogb[4 + ch]], [pqb], ch == 3)
                act(T[0][:, c0:c0 + n], pm[:, 0:n], AF.Copy, [pmb], [Tb[0]])
                act(T[1][:, c0:c0 + n], pm[:, 0:n], AF.Square, [pmb], [Tb[1]])
                tt("dve", T[1][:, c0:c0 + n], pq[:, 0:n], T[1][:, c0:c0 + n], ALU.subtract, [pqb, Tb[1]], [Tb[1]])
                ts("pool", T[1][:, c0:c0 + n], T[1][:, c0:c0 + n], LN_EPS, None, ALU.add, None, [Tb[1]], [Tb[1]])
                tt("pool", T[1][:, c0:c0 + n], T[1][:, c0:c0 + n], negh[:, c0:c0 + n], ALU.pow, [Tb[1], miscb], [Tb[1]])
            for ch in range(4):
                tt("dve", T[2][:, 0:W], T[5 + ch][:, 0:W], T[0][:, 0:W], ALU.subtract, [Tb[5 + ch], Tb[0]], [Tb[2]])
                tt("pool", T[2][:, 0:W], T[2][:, 0:W], T[1][:, 0:W], ALU.mult, [Tb[2], Tb[1]], [Tb[2]])
                act(T[3][:, 0:W], T[2][:, 0:W], AF.Identity, [Tb[2], dconb], [Tb[3]], scale=dc("clgh", ch), bias=dc("clbh", ch))
                act(T[4][:, 0:W], T[3][:, 0:W], AF.Tanh, [Tb[3]], [Tb[4]])
                stt(cact[:, ch, 0:W], T[4][:, 0:W], 1.0, T[3][:, 0:W], ALU.add, ALU.mult, [Tb[4], Tb[3]], [cactb])

        S.add("pool", lambda e: e.memset(T[9][:], 1.0), writes=[Tb[9]])
        ONES = T[9]
        ONESb = Tb[9]

        def hgrn_front2(W, hg, hl, uf, uq, need_q):
            h = hg * 4 + hl
            nch = W // 64
            gs = groups_of(W)
            for g in gs:
                c0, n = g
                pf, pfb = proj_fm(uf, hl, g)
                act(T[0][:, c0:c0 + n], pf[:, 0:n], AF.Tanh, [pfb], [Tb[0]], scale=0.5)
            act(T[1][:, 0:W], T[0][:, 0:W], AF.Ln, [Tb[0], dconb], [Tb[1]], scale=dc("a0", h), bias=dc("c0", h))
            ts("dve", T[2][:, 0:W], T[0][:, 0:W], dc("na0", h), dc("a0", h), ALU.mult, ALU.add, [Tb[0], dconb], [Tb[2]])
            for c in range(nch):
                S.add("dve", lambda e, c=c: e.tensor_tensor_scan(out=T[3][:, c * 64:(c + 1) * 64], data0=ONES[:, c * 64:(c + 1) * 64],
                                                               data1=T[1][:, c * 64:(c + 1) * 64], initial=0.0, op0=ALU.mult, op1=ALU.add),
                      reads=[Tb[1], ONESb], writes=[Tb[3]])
            b3 = T[3][:, 0:W].rearrange("p (c j) -> p c j", j=64)
            t43 = T[4][:, 0:W].rearrange("p (c j) -> p c j", j=64)
            tt("dve", t43, b3[:, :, 63:64].broadcast_to([128, nch, 64]), b3, ALU.subtract, [Tb[3]], [Tb[4]])
            act(T[4][:, 0:W], T[4][:, 0:W], AF.Exp, [Tb[4]], [Tb[4]])
            tt("pool", T[5][:, 0:W], T[2][:, 0:W], T[4][:, 0:W], ALU.mult, [Tb[2], Tb[4]], [Tb[5]])
            act(dch[hl][:, 0:nch], b3[:, :, 63:64].rearrange("p c j -> p (c j)"), AF.Exp, [Tb[3]], [dchb[hl]])
            for p0 in range(0, W // 128, 4):
                ps, pb = next_ps()
                np_ = min(4, W // 128 - p0)
                for pp in range(np_):
                    p = p0 + pp
                    S.add("pe", lambda e, ps=ps, pp=pp, p=p: e.transpose(out=ps[:, pp * 128:(pp + 1) * 128], in_=T[5][:, p * 128:(p + 1) * 128], identity=cs("ident", 0, 128)),
                          reads=[Tb[5], cstb], writes=[pb], signal=(pp == np_ - 1))
                cp("act", Ktok[hl][:, p0 * 128:(p0 + np_) * 128], ps[:, 0:np_ * 128], [pb], [Ktokb[hl]])
            if need_q:
                act(dmid[hl][:, 0:nch], b3[:, :, 31:32].rearrange("p c j -> p (c j)"), AF.Exp, [Tb[3]], [dchb[hl]])
                tt("dve", t43, b3, b3[:, :, 31:32].broadcast_to([128, nch, 64]), ALU.subtract, [Tb[3]], [Tb[4]])
                act(T[6][:, 0:W], T[4][:, 0:W], AF.Exp, [Tb[4]], [Tb[6]], bias=float(np.log(0.5)))
                act(T[4][:, 0:W], T[4][:, 0:W], AF.Exp, [Tb[4]], [Tb[4]], scale=-1.0)
                tt("pool", Kb[hl][:, 0:W], T[2][:, 0:W], T[4][:, 0:W], ALU.mult, [Tb[2], Tb[4]], [Kbb[hl]])
                for g in gs:
                    c0, n = g
                    pq, pqb = proj_fm(uq, hl, g)
                    act(T[0][:, c0:c0 + n], pq[:, 0:n], AF.Tanh, [pqb], [Tb[0]], scale=0.5)
                    stt(T[0][:, c0:c0 + n], T[0][:, c0:c0 + n], 1.0, pq[:, 0:n], ALU.add, ALU.mult, [Tb[0], pqb], [Tb[0]])
                tt("dve", Qb[hl][:, 0:W], T[0][:, 0:W], T[6][:, 0:W], ALU.mult, [Tb[0], Tb[6]], [Qbb[hl]])

        def v_stage(W, ui):
            for p in range(W // 128):
                ps, pb = next_ps()
                for k in range(8):
                    r, rb = wsl(ui, k, 0, 512)
                    mm(ps[:, :], xT[:, k, p * 128:(p + 1) * 128], r, k == 0, k == 7, [xTb, rb], [pb], k == 7)
                cp(alt2(), vtok[:, p, :], ps[:, :], [pb], [vtokb])

        def state_update(hg, hl, p, c):
            h = hg * 4 + hl
            rows = slice(c * 64, (c + 1) * 64)
            pd, pdb = next_ps()
            mm(pd[:, 0:128], Ktok[hl][rows, p * 128:(p + 1) * 128], vtok[rows, p, hl * 128:(hl + 1) * 128], True, True,
               [Ktokb[hl], vtokb], [pdb], True)
            stt(Sst[:, h, :], Sst[:, h, :], dch[hl][:, 2 * p + c:2 * p + c + 1], pd[:, 0:128], ALU.mult, ALU.add,
                [Sstb[h], dchb[hl], pdb], [Sstb[h]])

        def hgrn_out_stage(W, hg, hl, po, pob, ug):
            h = hg * 4 + hl
            gs = groups_of(W)
            for gi, (c0, n) in enumerate(gs):
                ob, obb = po[hl][gi], pob[hl][gi]
                act(osq[:, c0:c0 + n], ob[:, 0:n], AF.Square, [obb], [osqb])
                pm, pmb = next_ps()
                mm(pm[:, 0:n], onesH[:], osq[:, c0:c0 + n], True, True, [miscb, osqb], [pmb], True)
                ts("dve", T[1][:, c0:c0 + n], pm[:, 0:n], RMS_EPS, None, ALU.add, None, [pmb], [Tb[1]])
                tt("pool", T[1][:, c0:c0 + n], T[1][:, c0:c0 + n], negh[:, c0:c0 + n], ALU.pow, [Tb[1], miscb], [Tb[1]])
                tt("dve", T[2][:, c0:c0 + n], ob[:, 0:n], T[1][:, c0:c0 + n], ALU.mult, [obb, Tb[1]], [Tb[2]])
                pg, pgb = proj_fm(ug, hl, (c0, n))
                act(T[3][:, c0:c0 + n], pg[:, 0:n], AF.Tanh, [pgb], [Tb[3]], scale=0.5)
                stt(T[3][:, c0:c0 + n], T[3][:, c0:c0 + n], 1.0, pg[:, 0:n], ALU.add, ALU.mult, [Tb[3], pgb], [Tb[3]])
                stt(og[:, h, c0:c0 + n], T[2][:, c0:c0 + n], dc("gnh", h), T[3][:, c0:c0 + n], ALU.mult, ALU.mult,
                    [Tb[2], Tb[3], dconb], [ogb[h]])

        def hgrn_stage(W, full):
            gs = groups_of(W)
            for hg in range(2):
                uf = take(f"f{hg}")
                if full:
                    uq = take(f"q{hg}")
                else:
                    uq = None
                for hl in range(4):
                    hgrn_front2(W, hg, hl, uf, uq, full)
                rel()
                ui = take(f"i{hg}")
                v_stage(W, ui)
                if full:
                    ug = take(f"g{hg}")
                    for sub in range(2):
                        po = {}
                        pob = {}
                        for hl in (2 * sub, 2 * sub + 1):
                            po[hl] = []
                            pob[hl] = []
                            for _ in gs:
                                a, b_ = next_ps(hold=True)
                                po[hl].append(a)
                                pob[hl].append(b_)
                        hgrn_scan_sub(W, hg, (2 * sub, 2 * sub + 1), po, pob)
                        for hl in (2 * sub, 2 * sub + 1):
                            hgrn_out_stage(W, hg, hl, po, pob, ug)
                            for b_ in pob[hl]:
                                unhold(b_)
                    rel()
                else:
                    rel()
                    for p in range(W // 128):
                        for c in range(2):
                            for hl in range(4):
                                state_update(hg, hl, p, c)

        def hgrn_scan_sub(W, hg, hls, po, pob):
            gs = groups_of(W)
            for p in range(W // 128):
                sm_of = {}
                for hl in hls:
                    pss, pssb = next_ps()
                    mm(pss[:, 0:128], Kb[hl][:, p * 128:(p + 1) * 128], Qb[hl][:, p * 128:(p + 1) * 128], True, True,
                       [Kbb[hl], Qbb[hl]], [pssb], True)
                    si = state["smk"]
                    state["smk"] ^= 1
                    tt("dve", smk[si][:], pss[:, 0:128], maskb[:], ALU.mult, [pssb, miscb], [smkb[si]])
                    sm_of[hl] = si
                for c in range(2):
                    for hl in hls:
                        h = hg * 4 + hl
                        si = sm_of[hl]
                        col = p * 128 + c * 64
                        gi = 0
                        for j, (c0, n) in enumerate(gs):
                            if c0 <= col < c0 + n:
                                gi = j
                        c0 = gs[gi][0]
                        ob, obb = po[hl][gi], pob[hl][gi]
                        rows = slice(c * 64, (c + 1) * 64)
                        act(Sbf[:, h, :], Sst[:, h, :], AF.Identity, [Sstb[h], dchb[hl]], [Sbfb[h]], scale=dmid[hl][:, 2 * p + c:2 * p + c + 1])
                        mm(ob[:, col - c0:col - c0 + 64], vtok[rows, p, hl * 128:(hl + 1) * 128], smk[si][rows, c * 64:(c + 1) * 64],
                           True, False, [vtokb, smkb[si]], [obb], False)
                        mm(ob[:, col - c0:col - c0 + 64], Sbf[:, h, :], Qb[hl][:, col:col + 64], False, True,
                           [Sbfb[h], Qbb[hl]], [obb], True)
                        state_update(hg, hl, p, c)

        def merge_stage(W):
            gs = groups_of(W)
            for half in range(2):
                um0 = take(f"m0_{half}")
                uco = take(f"co{half}")
                for j in range(4):
                    for g in gs:
                        c0, n = g
                        pm0, pm0b = proj_fm(um0, j, g)
                        act(T[4][:, c0:c0 + n], pm0[:, 0:n], AF.Tanh, [pm0b], [Tb[4]], scale=0.5)
                        pyc, pycb = proj_fm(uco, j, g, kn=4, src=cact, srcb=[cactb])
                        stt(T[j][:, c0:c0 + n], T[4][:, c0:c0 + n], 1.0, pyc[:, 0:n], ALU.add, ALU.mult, [Tb[4], pycb], [Tb[j]])
                rel()
                um1 = take(f"m1_{half}")
                uho = take(f"ho{half}")
                for j in range(4):
                    jj = half * 4 + j
                    for g in gs:
                        c0, n = g
                        pm1, pm1b = proj_fm(um1, j, g)
                        act(T[4][:, c0:c0 + n], pm1[:, 0:n], AF.Tanh, [pm1b], [Tb[4]], scale=0.5)
                        pyh, pyhb = proj_fm(uho, j, g, kn=8, src=og, srcb=ogb)
                        stt(T[5][:, c0:c0 + n], T[4][:, c0:c0 + n], 1.0, pyh[:, 0:n], ALU.add, ALU.mult, [Tb[4], pyhb], [Tb[5]])
                        tt("pool", mixT[:, jj, c0:c0 + n], T[j][:, c0:c0 + n], T[5][:, c0:c0 + n], ALU.add, [Tb[j], Tb[5]], [mixTb])
                rel()

        def layer_norm_tok(buf, bufb, gi, bi):
            for hh in range(2):
                S.add("dve", lambda e, hh=hh: e.bn_stats(out=bst[:, hh, :], in_=buf[:, hh * 512:(hh + 1) * 512]), reads=[bufb], writes=[bstb])
            S.add("dve", lambda e: e.bn_aggr(out=mv[:], in_=bst[:].rearrange("p a b -> p (a b)")), reads=[bstb], writes=[mvb])
            ts("pool", mv[:, 1:2], mv[:, 1:2], LN_EPS, None, ALU.add, None, [mvb], [mvb])
            tt("pool", mv[:, 1:2], mv[:, 1:2], dc("negh"), ALU.pow, [mvb, dconb], [mvb])
            ts("dve", buf[:], buf[:], mv[:, 0:1], mv[:, 1:2], ALU.subtract, ALU.mult, [bufb, mvb], [bufb])
            tt("pool", buf[:], buf[:], lnt[:, gi, :], ALU.mult, [bufb, lntb], [bufb])
            tt("dve", buf[:], buf[:], lnt[:, bi, :], ALU.add, [bufb, lntb], [bufb])

        def wout_ln1_stage(row0, W):
            uw = [take("wo0"), take("wo1")]
            npair = W // 128
            halo = (W == 640)
            for p in range(npair):
                if halo and p == 0:
                    dst, dstb = x1h, x1hb
                else:
                    mp = p - (1 if halo else 0)
                    dst, dstb = x1tok[mp], x1tokb[mp]
                sem = xtoksem[0] if (halo and p == 0) else x1sem[mp]
                S.add("pool", lambda e, dst=dst, r=row0 + p * 128: e.dma_start(out=dst[:], in_=xc[r:r + 128, :]),
                      writes=[dstb], dsem=sem)
                ts("pool", dst[:], dst[:], ALPHA, None, ALU.mult, None, [dstb], [dstb])
                for ch2 in range(2):
                    ps, pb = next_ps()
                    for k in range(8):
                        r, rb = wsl(uw[ch2], k, 0, 512)
                        mm(ps[:, :], mixT[:, k, p * 128:(p + 1) * 128], r, k == 0, k == 7, [mixTb, rb], [pb], k == 7)
                    stt(dst[:, ch2 * 512:(ch2 + 1) * 512], ps[:, :], 0.5, dst[:, ch2 * 512:(ch2 + 1) * 512], ALU.mult, ALU.add, [pb, dstb], [dstb])
                layer_norm_tok(dst, dstb, 0, 1)
            rel()
            for p in range(npair):
                if halo and p == 0:
                    src, srcb = x1h, x1hb
                else:
                    mp = p - (1 if halo else 0)
                    src, srcb = x1tok[mp], x1tokb[mp]
                for kb in range(2):
                    ps, pb = next_ps()
                    for kk in range(4):
                        k = kb * 4 + kk
                        S.add("pe", lambda e, ps=ps, kk=kk, k=k, src=src: e.transpose(out=ps[:, kk * 128:(kk + 1) * 128], in_=src[:, k * 128:(k + 1) * 128], identity=cs("ident", 0, 128)),
                              reads=[srcb, cstb], writes=[pb], signal=(kk == 3))
                    cp(alt2(), mixT[:, kb * 4:kb * 4 + 4, p * 128:(p + 1) * 128], ps[:, :].rearrange("p (k c) -> p k c", c=128), [pb], [mixTb])

        def ffn_in_stage(W):
            halo = (W == 640)
            m0 = 128 if halo else 0
            gs = groups_of(W)
            for j in range(6):
                uu = take(f"u{j}")
                ugv = take(f"gv{j}")
                ncl = 4 if j < 5 else 2
                for cl in range(ncl):
                    hc = j * 4 + cl
                    ui = hc % 2
                    ub, ubb = ubuf[ui], ubufb[ui]
                    cp("pool", ub[:, 0:2], uhist[:, hc, :], [uhistb], [ubb])
                    for g in gs:
                        c0, n = g
                        pu, pub = proj_fm(uu, cl, g, src=mixT, srcb=[mixTb])
                        cp("act", ub[:, 2 + c0:2 + c0 + n], pu[:, 0:n], [pub], [ubb])
                    if halo:
                        ts("pool", ub[:, 2:130], ub[:, 2:130], cs("flag", 0, 1), None, ALU.mult, None, [ubb, cstb], [ubb])
                    cp("pool", uhist[:, hc, :], ub[:, W:W + 2], [ubb], [uhistb])
                    pc, pcb = next_ps()
                    for k in range(3):
                        di = state["dg"]
                        state["dg"] = (di + 1) % NDG
                        ts("pool", dg[di][:], cs("ident", 0, 128), cs("ffw", hc * 3 + k, 1), None, ALU.mult, None, [cstb], [dgb[di]])
                        mm(pc[:, :], dg[di][:], ub[:, m0 + k:m0 + k + 512], k == 0, k == 2, [dgb[di], ubb], [pcb], k == 2)
                    act(T[0][:, 0:512], pc[:, :], AF.Square, [pcb, cstb], [Tb[0]], bias=cs("ffb", hc, 1))
                    act(T[1][:, 0:512], pc[:, :], AF.Identity, [pcb, cstb], [Tb[1]], bias=cs("ffb", hc, 1))
                    ts("dve", T[0][:, 0:512], T[0][:, 0:512], GC2, GC1, ALU.mult, ALU.add, [Tb[0]], [Tb[0]])
                    tt("pool", T[0][:, 0:512], T[0][:, 0:512], T[1][:, 0:512], ALU.mult, [Tb[0], Tb[1]], [Tb[0]])
                    act(T[0][:, 0:512], T[0][:, 0:512], AF.Tanh, [Tb[0]], [Tb[0]])
                    stt(T[0][:, 0:512], T[0][:, 0:512], 1.0, T[1][:, 0:512], ALU.add, ALU.mult, [Tb[0], Tb[1]], [Tb[0]])
                    pg, pgb = proj_fm(ugv, cl, (m0, 512), src=mixT, srcb=[mixTb])
                    stt(hT[:, hc, :], T[0][:, 0:512], 0.5, pg[:, :], ALU.mult, ALU.mult, [Tb[0], pgb], [hTb[hc]])
                rel()

        def ffn_out_stage(orow0):
            for ch2 in range(2):
                ufo = take(f"fo{ch2}")
                pss = [next_ps(hold=True) for _ in range(4)]
                for hc in range(NFC):
                    r, rb = wsl(ufo, hc, 0, 512)
                    for p in range(4):
                        ps, pb = pss[p]
                        mm(ps[:, :], hT[:, hc, p * 128:(p + 1) * 128], r, hc == 0, hc == NFC - 1, [hTb[hc], rb], [pb], hc == NFC - 1)
                for p in range(4):
                    ps, pb = pss[p]
                    stt(x1tok[p][:, ch2 * 512:(ch2 + 1) * 512], x1tok[p][:, ch2 * 512:(ch2 + 1) * 512], ALPHA, ps[:, :], ALU.mult, ALU.add,
                        [x1tokb[p], pb], [x1tokb[p]])
                    unhold(pb)
                rel()
            for p in range(4):
                layer_norm_tok(x1tok[p], x1tokb[p], 2, 3)
                S.add("pool", lambda e, p=p, r=orow0 + p * 128: e.dma_start(out=outd[r:r + 128, :], in_=x1tok[p][:]),
                      reads=[x1tokb[p]], writes=[outb[p]], dsem=outsem[p])

        for kind, row0, W, flag in tiles:
            load_x(row0, W)
            if kind == "pre":
                if flag:
                    glu_stage(W)
                    cbuf_carry(W)
                hgrn_stage(W, False)
            else:
                glu_stage(W)
                conv_stage(W)
                cbuf_carry(W)
                hgrn_stage(W, True)
                merge_stage(W)
                wout_ln1_stage(row0, W)
                ffn_in_stage(W)
                orow0 = (row0 + 128 - HALF) if W == 640 else (row0 - HALF)
                ffn_out_stage(orow0)
        S.final_waits("pool", outb)
        assert wpos["p"] == len(WS.seq)
        S.emit(nc)
    return nc


def _pack_weights(mats):
    wp = np.zeros((NUNITS, 128, 4, 512), np.float32)
    for name in UNAMES:
        for i, (mat, r0, nr, c0, ncol) in enumerate(UT[name]):
            blk = mats[mat][r0:r0 + nr, c0:c0 + ncol]
            nk = nr // 128
            wp[UBASE[name] + i, :, :nk, :ncol] = blk.reshape(nk, 128, ncol).transpose(1, 0, 2)
    return wp.reshape(NUNITS, 128, UC)


def _pack_consts(inp, flag):
    c = np.zeros((128, NCONST), np.float32)

    def put(name, arr):
        o, w = CO[name]
        assert arr.shape == (128, w), (name, arr.shape)
        c[:, o:o + w] = arr
    cw = inp["w_conv_dw"][0]
    put("convw", cw.T.reshape(4, 128, CONV_K).transpose(1, 0, 2).reshape(128, 4 * CONV_K))
    put("convb", inp["b_conv_dw"][0].reshape(4, 128).T)
    put("clg", inp["conv_ln_g"][0].reshape(4, 128).T)
    put("clb", inp["conv_ln_b"][0].reshape(4, 128).T)
    put("l0", inp["hgrn_lb_logits"][0].reshape(8, 128).T)
    put("l1", inp["hgrn_lb_logits"][1].reshape(8, 128).T)
    put("gn", inp["hgrn_norm_g"][0].reshape(8, 128).T)
    fw = inp["w_ffn_dw"][0]
    put("ffw", fw.T.reshape(NFC, 128, 3).transpose(1, 0, 2).reshape(128, NFC * 3))
    put("ffb", inp["b_ffn_dw"][0].reshape(NFC, 128).T)
    put("flag", np.full((128, 1), flag, np.float32))
    put("ident", np.eye(128, dtype=np.float32))
    s = np.arange(128)[:, None]
    t = np.arange(128)[None, :]
    put("mask", ((s <= t) & ((s // 64) == (t // 64))).astype(np.float32))
    return c


_CACHE = {}


def kernel(x, w_in, w_conv_dw, b_conv_dw, conv_ln_g, conv_ln_b, w_conv_out, hgrn_lb_logits, hgrn_norm_g,
           w_hgrn_out, w_out, ln1_g, ln1_b, w_ffn_in, w_ffn_dw, b_ffn_dw, w_ffn_out, ln2_g, ln2_b):
    inp = dict(w_conv_dw=np.asarray(w_conv_dw, np.float32), b_conv_dw=np.asarray(b_conv_dw, np.float32),
               conv_ln_g=np.asarray(conv_ln_g, np.float32), conv_ln_b=np.asarray(conv_ln_b, np.float32),
               hgrn_lb_logits=np.asarray(hgrn_lb_logits, np.float32), hgrn_norm_g=np.asarray(hgrn_norm_g, np.float32),
               w_ffn_dw=np.asarray(w_ffn_dw, np.float32), b_ffn_dw=np.asarray(b_ffn_dw, np.float32))
    mats = {"w_in": np.asarray(w_in, np.float32)[0], "w_conv_out": np.asarray(w_conv_out, np.float32)[0],
            "w_hgrn_out": np.asarray(w_hgrn_out, np.float32)[0], "w_out": np.asarray(w_out, np.float32)[0],
            "w_ffn_in": np.asarray(w_ffn_in, np.float32)[0], "w_ffn_out": np.asarray(w_ffn_out, np.float32)[0]}
    x = np.asarray(x, np.float32)
    wp = _pack_weights(mats)
    lnp = np.stack([np.broadcast_to(np.asarray(a, np.float32)[0][None, :], (128, D)) for a in (ln1_g, ln1_b, ln2_g, ln2_b)], axis=1)
    lnp = np.ascontiguousarray(lnp)
    if "nc" not in _CACHE:
        _CACHE["nc"] = build_program()
    nc = _CACHE["nc"]
    in_maps = []
    for core in range(8):
        b, half = core // 2, core % 2
        if half == 0:
            xcore = np.concatenate([np.zeros((HALF, D), np.float32), x[b, :HALF]], axis=0)
        else:
            xcore = x[b]
        in_maps.append({"xc": np.ascontiguousarray(xcore), "wpack": wp, "cpack": _pack_consts(inp, float(half)), "lnpack": lnp})
    res = run_bass_kernel_spmd(nc, in_maps, core_ids=list(range(8)))
    out = np.zeros((NB, SEQ, D), np.float32)
    for core in range(8):
        b, half = core // 2, core % 2
        out[b, half * HALF:(half + 1) * HALF] = res.results[core]["out"]
    return out
```

```python
import contextlib
import numpy as np
import concourse.bass as bass
import concourse.mybir as mybir
from concourse.bass_utils import run_bass_kernel_spmd

F32 = mybir.dt.float32
BF16 = mybir.dt.bfloat16
AF = mybir.ActivationFunctionType
ALU = mybir.AluOpType

D = 1024
SEQ = 8192
NB = 4
HALF = 4096
CONV_DIM = 512
CONV_K = 31
HG = 1024
D_FF = 2816
NFC = 22
LN_EPS = 1e-5
RMS_EPS = 1e-6
ALPHA = 2.0 ** 0.25
GC1 = 0.7978845608028654
GC2 = GC1 * 0.044715
UC = 2048

ENGS = ["pe", "act", "dve", "pool", "sp"]


class Buf:
    __slots__ = ("name", "w", "r")

    def __init__(self, name):
        self.name = name
        self.w = None
        self.r = []


class DmaSem:
    __slots__ = ("name", "count", "h")

    def __init__(self, name):
        self.name = name
        self.count = 0
        self.h = None


class Sched:
    def __init__(self):
        self.ops = {e: [] for e in ENGS}
        self.cnt = {e: 0 for e in ENGS}
        self.waited = {e: {} for e in ENGS}
        self.pending = {e: ([], []) for e in ENGS}
        self.dsems = []

    def dma_sem(self, name):
        s = DmaSem(name)
        self.dsems.append(s)
        return s

    def add(self, eng, fn, reads=(), writes=(), signal=True, dsem=None):
        deps = {}

        def need(ev):
            if ev is None:
                return
            k, v = ev
            if k == "pe" and eng == "pe":
                return
            if deps.get(k, 0) < v:
                deps[k] = v
        for b in reads:
            need(b.w)
        for b in writes:
            need(b.w)
            for ev in b.r:
                need(ev)
        waits = []
        for k, v in deps.items():
            if self.waited[eng].get(k, 0) >= v:
                continue
            self.waited[eng][k] = v
            waits.append((k, v))
        ev = None
        if dsem is not None:
            dsem.count += 16
            ev = (dsem, dsem.count)
            for b in reads:
                b.r.append(ev)
            for b in writes:
                b.w = ev
                b.r = []
        else:
            pr, pw = self.pending[eng]
            pr.extend(reads)
            pw.extend(writes)
            if signal:
                self.cnt[eng] += 1
                ev = (eng, self.cnt[eng])
                for b in pr:
                    b.r.append(ev)
                for b in pw:
                    b.w = ev
                    b.r = []
                self.pending[eng] = ([], [])
        self.ops[eng].append((waits, fn, ev))
        return ev

    def final_waits(self, eng, bufs):
        deps = {}
        for b in bufs:
            if b.w is not None:
                k, v = b.w
                deps[k] = max(deps.get(k, 0), v)
        self.ops[eng].append((list(deps.items()), None, None))

    def emit(self, nc):
        for e in ENGS:
            pr, pw = self.pending[e]
            assert not pr and not pw, f"pending unsignaled ops on {e}"
        with contextlib.ExitStack() as st:
            sems = {}
            for e in ENGS:
                sems[e] = st.enter_context(nc.semaphore(f"s_{e}"))
            for d in self.dsems:
                d.h = st.enter_context(nc.semaphore(f"d_{d.name}"))
            block = st.enter_context(nc.Block())

            def run(engobj, lst):
                for waits, fn, ev in lst:
                    for k, v in waits:
                        h = sems[k] if isinstance(k, str) else k.h
                        engobj.wait_ge(h, v)
                    if fn is None:
                        continue
                    ins = fn(engobj)
                    if ev is not None:
                        k, v = ev
                        if isinstance(k, str):
                            ins.then_inc(sems[k], 1)
                        else:
                            ins.then_inc(k.h, 16)

            @block.tensor
            def _(e):
                run(e, self.ops["pe"])

            @block.scalar
            def _(e):
                run(e, self.ops["act"])

            @block.vector
            def _(e):
                run(e, self.ops["dve"])

            @block.gpsimd
            def _(e):
                run(e, self.ops["pool"])

            @block.sync
            def _(e):
                run(e, self.ops["sp"])


IN_OFF = {"cv": 0, "cg": 512, "q": 1024, "f": 2048, "i": 3072, "g": 4096, "m0": 5120, "m1": 6144}


def unit_table():
    t = {}

    def add(name, mat, k_rows, col0, ncols):
        us = []
        for r0 in range(0, k_rows, 512):
            us.append((mat, r0, min(512, k_rows - r0), col0, ncols))
        t[name] = us
    add("cv", "w_in", D, 0, 512)
    add("cg", "w_in", D, 512, 512)
    for hg in range(2):
        for nm in ["f", "q", "i", "g"]:
            add(f"{nm}{hg}", "w_in", D, IN_OFF[nm] + hg * 512, 512)
        add(f"m0_{hg}", "w_in", D, IN_OFF["m0"] + hg * 512, 512)
        add(f"m1_{hg}", "w_in", D, IN_OFF["m1"] + hg * 512, 512)
        add(f"co{hg}", "w_conv_out", CONV_DIM, hg * 512, 512)
        add(f"ho{hg}", "w_hgrn_out", HG, hg * 512, 512)
        add(f"wo{hg}", "w_out", D, hg * 512, 512)
        add(f"fo{hg}", "w_ffn_out", D_FF, hg * 512, 512)
    for j in range(6):
        nc_ = 512 if j < 5 else 256
        add(f"u{j}", "w_ffn_in", D, j * 512, nc_)
        add(f"gv{j}", "w_ffn_in", D, D_FF + j * 512, nc_)
    return t


UT = unit_table()
UNAMES = list(UT.keys())
UBASE = {}
_n = 0
for _k in UNAMES:
    UBASE[_k] = _n
    _n += len(UT[_k])
NUNITS = _n

CO = {}
_c = 0
for _nm, _w in [("convw", 4 * CONV_K), ("convb", 4), ("clg", 4), ("clb", 4), ("l0", 8), ("l1", 8), ("gn", 8),
                ("ffw", NFC * 3), ("ffb", NFC), ("flag", 1), ("ident", 128), ("mask", 128)]:
    CO[_nm] = (_c, _w)
    _c += _w
NCONST = _c


def build_program():
    nc = bass.Bass("TRN2", target_bir_lowering=False)
    xc = nc.dram_tensor("xc", [2 * HALF, D], F32, kind="ExternalInput").ap()
    wpack = nc.dram_tensor("wpack", [NUNITS, 128, UC], F32, kind="ExternalInput").ap()
    cpack = nc.dram_tensor("cpack", [128, NCONST], F32, kind="ExternalInput").ap()
    lnpack = nc.dram_tensor("lnpack", [128, 4, D], F32, kind="ExternalInput").ap()
    outd = nc.dram_tensor("out", [HALF, D], F32, kind="ExternalOutput").ap()

    S = Sched()
    st = contextlib.ExitStack()
    with st:
        def sb(name, shape, dt):
            return st.enter_context(nc.sbuf_tensor(name, shape, dt))

        WMAX = 640
        NRING = 6
        ring = [sb(f"ring{i}", [128, 4, 512], BF16) for i in range(NRING)]
        ringb = [Buf(f"ring{i}") for i in range(NRING)]
        NSTG = 2
        stg = [sb(f"stg{i}", [128, 4, 512], F32) for i in range(NSTG)]
        stgb = [Buf(f"stg{i}") for i in range(NSTG)]
        stgsem = [S.dma_sem(f"stg{i}") for i in range(NSTG)]
        cst = sb("cst", [128, NCONST], F32)
        cstb = Buf("cst")
        lnt = sb("lnt", [128, 4, D], F32)
        lntb = Buf("lnt")
        xT = sb("xT", [128, 8, WMAX], BF16)
        xTb = Buf("xT")
        xtok = [sb(f"xtok{i}", [128, D], F32) for i in range(2)]
        xtokb = [Buf(f"xtok{i}") for i in range(2)]
        xtoksem = [S.dma_sem(f"xtok{i}") for i in range(2)]
        x1h = xtok[0]
        x1hb = xtokb[0]
        x1sem = [S.dma_sem(f"x1r{i}") for i in range(4)]
        cbuf = sb("cbuf", [128, 4, 32 + WMAX], BF16)
        cbufb = [Buf(f"cbuf{i}") for i in range(4)]
        cact = sb("cact", [128, 4, WMAX], BF16)
        cactb = Buf("cact")
        og = sb("og", [128, 8, WMAX], BF16)
        ogb = [Buf(f"og{i}") for i in range(8)]
        mixT = sb("mixT", [128, 8, WMAX], BF16)
        mixTb = Buf("mixT")
        x1tok = [sb(f"x1tok{i}", [128, D], F32) for i in range(4)]
        x1tokb = [Buf(f"x1tok{i}") for i in range(4)]
        outsem = [S.dma_sem(f"out{i}") for i in range(4)]
        outb = [Buf(f"outd{i}") for i in range(4)]
        hT = sb("hT", [128, NFC, 512], BF16)
        hTb = [Buf(f"hT{i}") for i in range(NFC)]
        vtok = sb("vtok", [128, 5, 512], BF16)
        vtokb = Buf("vtok")
        Sst = sb("Sst", [128, 8, 128], F32)
        Sbf = sb("Sbf", [128, 8, 128], BF16)
        Sstb = [Buf(f"S{i}") for i in range(8)]
        Sbfb = [Buf(f"Sb{i}") for i in range(8)]
        NT = 10
        T = [sb(f"T{i}", [128, WMAX], F32) for i in range(NT)]
        Tb = [Buf(f"T{i}") for i in range(NT)]
        Qb = [sb(f"Qb{i}", [128, WMAX], BF16) for i in range(4)]
        Kb = [sb(f"Kb{i}", [128, WMAX], BF16) for i in range(4)]
        Ktok = [sb(f"Ktok{i}", [128, WMAX], BF16) for i in range(4)]
        dch = [sb(f"dch{i}", [128, 16], F32) for i in range(4)]
        dmid = [sb(f"dmid{i}", [128, 16], F32) for i in range(4)]
        Qbb = [Buf(f"Qb{i}") for i in range(4)]
        Kbb = [Buf(f"Kb{i}") for i in range(4)]
        Ktokb = [Buf(f"Ktok{i}") for i in range(4)]
        dchb = [Buf(f"dch{i}") for i in range(4)]
        osq = sb("osq", [128, WMAX], BF16)
        osqb = Buf("osq")
        smk = [sb(f"smk{i}", [128, 128], BF16) for i in range(2)]
        smkb = [Buf(f"smk{i}") for i in range(2)]
        NDG = 8
        dg = [sb(f"dg{i}", [128, 128], BF16) for i in range(NDG)]
        dgb = [Buf(f"dg{i}") for i in range(NDG)]
        ubuf = [sb(f"ubuf{i}", [128, 2 + WMAX], BF16) for i in range(2)]
        ubufb = [Buf(f"ubuf{i}") for i in range(2)]
        uhist = sb("uhist", [128, NFC, 2], BF16)
        uhistb = Buf("uhist")
        dcon = sb("dcon", [128, 64], F32)
        dconb = Buf("dcon")
        onesC = sb("onesC", [128, 128], BF16)
        onesH = sb("onesH", [128, 128], BF16)
        identb = sb("identb", [128, 128], BF16)
        maskb = sb("maskb", [128, 128], BF16)
        negh = sb("negh", [128, WMAX], F32)
        miscb = Buf("misc")
        bst = sb("bst", [128, 2, 6], F32)
        mv = sb("mv", [128, 2], F32)
        bstb = Buf("bst")
        mvb = Buf("mv")
        psum = [st.enter_context(nc.psum_tensor(f"ps{i}", [128, 512], F32)) for i in range(8)]
        psb = [Buf(f"ps{i}") for i in range(8)]

        state = {"ps": 0, "ring": 0, "stg": 0, "dg": 0, "xtok": 0, "smk": 0, "alt": 0}

        held = set()

        def next_ps(hold=False):
            for _ in range(8):
                i = state["ps"]
                state["ps"] = (i + 1) % 8
                if i not in held:
                    if hold:
                        held.add(i)
                    return psum[i], psb[i]
            raise RuntimeError("no free psum bank")

        def unhold(pb):
            held.discard(psb.index(pb))

        def cs(name, j=0, n=1):
            o, w = CO[name]
            return cst[:, o + j:o + j + n]

        DC = {"a0": 0, "na0": 8, "c0": 16, "gnh": 24, "clgh": 32, "clbh": 36, "eps_ln": 40, "eps_rms": 41, "negh": 42}

        def dc(name, j=0):
            o = DC[name]
            return dcon[:, o + j:o + j + 1]

        def act(out, in_, func, reads, writes, scale=None, bias=None):
            kw = {}
            if scale is not None:
                kw["scale"] = scale
            if bias is not None:
                kw["bias"] = bias
            S.add("act", lambda e: e.activation(out=out, in_=in_, func=func, **kw), reads=reads, writes=writes)

        def tt(eng, out, in0, in1, op, reads, writes):
            S.add(eng, lambda e: e.tensor_tensor(out=out, in0=in0, in1=in1, op=op), reads=reads, writes=writes)

        def ts(eng, out, in0, s1, s2, op0, op1, reads, writes):
            if op1 is None:
                S.add(eng, lambda e: e.tensor_scalar(out=out, in0=in0, scalar1=s1, scalar2=None, op0=op0), reads=reads, writes=writes)
            else:
                S.add(eng, lambda e: e.tensor_scalar(out=out, in0=in0, scalar1=s1, scalar2=s2, op0=op0, op1=op1), reads=reads, writes=writes)

        def stt(out, in0, scalar, in1, op0, op1, reads, writes):
            S.add("dve", lambda e: e.scalar_tensor_tensor(out=out, in0=in0, scalar=scalar, in1=in1, op0=op0, op1=op1), reads=reads, writes=writes)

        def mm(out, lhsT, rhs, start, stop, reads, writes, signal):
            S.add("pe", lambda e: e.matmul(out, lhsT=lhsT, rhs=rhs, start=start, stop=stop), reads=reads, writes=writes, signal=signal)

        def cp(eng, out, in_, reads, writes):
            if eng == "act":
                act(out, in_, AF.Copy, reads, writes)
            else:
                S.add(eng, lambda e: e.tensor_copy(out=out, in_=in_), reads=reads, writes=writes)

        def alt2():
            state["alt"] ^= 1
            return "act" if state["alt"] else "dve"

        csem = S.dma_sem("consts")
        S.add("sp", lambda e: e.dma_start(out=cst[:], in_=cpack), writes=[cstb], dsem=csem)
        S.add("sp", lambda e: e.dma_start(out=lnt[:], in_=lnpack), writes=[lntb], dsem=csem)
        S.add("pool", lambda e: e.memset(onesC[:], 1.0 / 512.0), writes=[miscb])
        S.add("pool", lambda e: e.memset(onesH[:], 1.0 / 128.0), writes=[miscb])
        S.add("pool", lambda e: e.memset(negh[:], -0.5), writes=[miscb])
        S.add("pool", lambda e: e.memset(Sst[:], 0.0), writes=Sstb)
        S.add("pool", lambda e: e.memset(Sbf[:], 0.0), writes=Sbfb)
        S.add("pool", lambda e: e.memset(uhist[:], 0.0), writes=[uhistb])
        S.add("pool", lambda e: e.memset(cbuf[:], 0.0), writes=cbufb)
        S.add("pool", lambda e: e.memset(dcon[:], 0.0), writes=[dconb])
        cp("dve", identb[:], cs("ident", 0, 128), [cstb], [miscb])
        cp("dve", maskb[:], cs("mask", 0, 128), [cstb], [miscb])
        tt("dve", dcon[:, 48:56], cs("l0", 0, 8), cs("l1", 0, 8), ALU.subtract, [cstb, dconb], [dconb])
        act(dcon[:, 48:56], dcon[:, 48:56], AF.Tanh, [dconb], [dconb], scale=0.5)
        ts("dve", dcon[:, 48:56], dcon[:, 48:56], 0.5, 0.5, ALU.mult, ALU.add, [dconb], [dconb])
        ts("dve", dcon[:, 0:8], dcon[:, 48:56], -0.5, 0.5, ALU.mult, ALU.add, [dconb], [dconb])
        ts("dve", dcon[:, 8:16], dcon[:, 48:56], 0.5, -0.5, ALU.mult, ALU.add, [dconb], [dconb])
        ts("dve", dcon[:, 16:24], dcon[:, 48:56], 0.5, 0.5, ALU.mult, ALU.add, [dconb], [dconb])
        ts("dve", dcon[:, 24:32], cs("gn", 0, 8), 0.5, None, ALU.mult, None, [cstb, dconb], [dconb])
        ts("dve", dcon[:, 32:36], cs("clg", 0, 4), 0.5, None, ALU.mult, None, [cstb, dconb], [dconb])
        ts("dve", dcon[:, 36:40], cs("clb", 0, 4), 0.5, None, ALU.mult, None, [cstb, dconb], [dconb])
        S.add("pool", lambda e: e.memset(dcon[:, 40:41], LN_EPS), reads=[dconb], writes=[dconb])
        S.add("pool", lambda e: e.memset(dcon[:, 41:42], RMS_EPS), reads=[dconb], writes=[dconb])
        S.add("pool", lambda e: e.memset(dcon[:, 42:43], -0.5), reads=[dconb], writes=[dconb])

        wq = []
        loaded = {}

        class WStream:
            def __init__(self):
                self.seq = []
                self.next_load = 0
                self.slot_of = {}

            def plan(self, units):
                self.seq.extend(units)

            def ensure(self, upto):
                while self.next_load < min(upto, len(self.seq)):
                    pos = self.next_load
                    u = self.seq[pos]
                    si = state["stg"]
                    state["stg"] = (si + 1) % NSTG
                    ri = state["ring"]
                    state["ring"] = (ri + 1) % NRING
                    S.add("sp", lambda e, u=u, si=si: e.dma_start(out=stg[si][:].rearrange("p a b -> p (a b)"), in_=wpack[u]),
                          writes=[stgb[si]], dsem=stgsem[si])
                    ce = ("pool", "act", "dve")[pos % 3]
                    cp(ce, ring[ri][:].rearrange("p a b -> p (a b)"), stg[si][:].rearrange("p a b -> p (a b)"), [stgb[si]], [ringb[ri]])
                    self.slot_of[pos] = ri
                    self.next_load += 1

        WS = WStream()
        wpos = {"p": 0, "live": 0}
        PREFETCH = 3

        def rel():
            wpos["live"] = wpos["p"]
            WS.ensure(wpos["p"] + PREFETCH)

        def take(name):
            n = len(UT[name])
            res = []
            for i in range(n):
                pos = wpos["p"]
                assert WS.seq[pos] == UBASE[name] + i, (name, i, pos)
                assert pos + 1 - wpos["live"] <= NRING, (name, pos, wpos["live"])
                WS.ensure(pos + 1)
                ri = WS.slot_of[pos]
                res.append((ring[ri], ringb[ri]))
                wpos["p"] += 1
            return res

        def wsl(units, k, c0, n):
            r, b = units[k // 4]
            return r[:, k % 4, c0:c0 + n], b

        def main_units():
            names = ["cv", "cg"]
            for hg in range(2):
                names += [f"f{hg}", f"q{hg}", f"i{hg}", f"g{hg}"]
            for hg in range(2):
                names += [f"m0_{hg}", f"co{hg}", f"m1_{hg}", f"ho{hg}"]
            names += ["wo0", "wo1"]
            for j in range(6):
                names += [f"u{j}", f"gv{j}"]
            names += ["fo0", "fo1"]
            return names

        def pre_units(last):
            names = []
            if last:
                names += ["cv", "cg"]
            for hg in range(2):
                names += [f"f{hg}", f"i{hg}"]
            return names

        tiles = []
        r0 = 0
        while r0 < HALF - 128:
            w = min(512, HALF - 128 - r0)
            tiles.append(("pre", r0, w, r0 + w >= HALF - 128))
            r0 += w
        tiles.append(("main", HALF - 128, 640, True))
        for i in range(1, 8):
            tiles.append(("main", HALF + 512 * i, 512, False))
        for kind, row0, W, flag in tiles:
            names = pre_units(flag) if kind == "pre" else main_units()
            for nm in names:
                WS.plan([UBASE[nm] + i for i in range(len(UT[nm]))])

        def groups_of(W):
            return [(0, 128), (128, 512)] if W == 640 else [(0, W)]

        def load_x(row0, W):
            for p in range(W // 128):
                xi = state["xtok"]
                state["xtok"] ^= 1
                S.add("pool", lambda e, xi=xi, r=row0 + p * 128: e.dma_start(out=xtok[xi][:], in_=xc[r:r + 128, :]),
                      writes=[xtokb[xi]], dsem=xtoksem[xi])
                for kb in range(2):
                    ps, pb = next_ps()
                    for kk in range(4):
                        k = kb * 4 + kk
                        S.add("pe", lambda e, ps=ps, kk=kk, k=k, xi=xi: e.transpose(out=ps[:, kk * 128:(kk + 1) * 128], in_=xtok[xi][:, k * 128:(k + 1) * 128], identity=cs("ident", 0, 128)),
                              reads=[xtokb[xi], cstb], writes=[pb], signal=(kk == 3))
                    cp(alt2(), xT[:, kb * 4:kb * 4 + 4, p * 128:(p + 1) * 128], ps[:, :].rearrange("p (k c) -> p k c", c=128), [pb], [xTb])

        def proj_fm(units, cl, g, kn=8, src=None, srcb=None):
            c0, n = g
            ps, pb = next_ps()
            src = xT if src is None else src
            srcb = [xTb] if srcb is None else srcb
            for k in range(kn):
                l, lb_ = wsl(units, k, cl * 128, 128)
                mm(ps[:, 0:n], l, src[:, k, c0:c0 + n], k == 0, k == kn - 1, [lb_] + srcb, [pb], k == kn - 1)
            return ps, pb

        def glu_stage(W):
            ucv = take("cv")
            ucg = take("cg")
            for ch in range(4):
                for g in groups_of(W):
                    c0, n = g
                    pv, pvb = proj_fm(ucv, ch, g)
                    pg, pgb = proj_fm(ucg, ch, g)
                    act(T[0][:, c0:c0 + n], pg[:, 0:n], AF.Tanh, [pgb], [Tb[0]], scale=0.5)
                    stt(cbuf[:, ch, 32 + c0:32 + c0 + n], T[0][:, c0:c0 + n], 1.0, pv[:, 0:n], ALU.add, ALU.mult, [Tb[0], pvb], [cbufb[ch]])
            rel()

        def cbuf_carry(W):
            for ch in range(4):
                cp("pool", cbuf[:, ch, 0:32], cbuf[:, ch, W:W + 32], [cbufb[ch]], [cbufb[ch]])

        def conv_stage(W):
            gs = groups_of(W)
            for ch in range(4):
                pss = [next_ps(hold=True) for _ in gs]
                for k in range(CONV_K):
                    tj, ts_ = k // 10, (k % 10) * 128
                    dgap = T[tj][:].bitcast(BF16)[:, ts_:ts_ + 128]
                    ts("dve", dgap, cs("ident", 0, 128), cs("convw", ch * CONV_K + k, 1), 0.5, ALU.mult, ALU.mult, [cstb], [Tb[tj]])
                    for gi_, ((c0, n), (ps, pb)) in enumerate(zip(gs, pss)):
                        mm(ps[:, 0:n], dgap, cbuf[:, ch, c0 + 2 + k:c0 + 2 + k + n], k == 0, k == CONV_K - 1,
                           [Tb[tj], cbufb[ch]], [pb], k == CONV_K - 1)
                for (c0, n), (ps, pb) in zip(gs, pss):
                    act(T[5 + ch][:, c0:c0 + n], ps[:, 0:n], AF.Identity, [pb, cstb], [Tb[5 + ch]], bias=cs("convb", ch, 1))
                    act(og[:, 4 + ch, c0:c0 + n], ps[:, 0:n], AF.Square, [pb, cstb], [ogb[4 + ch]], bias=cs("convb", ch, 1))
                    cp("dve", og[:, ch, c0:c0 + n], T[5 + ch][:, c0:c0 + n], [Tb[5 + ch]], [ogb[ch]])
                    unhold(pb)
            for (c0, n) in gs:
                pm, pmb = next_ps()
                for ch in range(4):
                    mm(pm[:, 0:n], onesC[:], og[:, ch, c0:c0 + n], ch == 0, ch == 3, [miscb, ogb[ch]], [pmb], ch == 3)
                pq, pqb = next_ps()
                for ch in range(4):
                    mm(pq[:, 0:n], onesC[:], og[:, 4 + ch, c0:c0 + n], ch == 0, ch == 3, [miscb, ogb[4 + ch]], [pqb], ch == 3)
                act(T[0][:, c0:c0 + n], pm[:, 0:n], AF.Copy, [pmb], [Tb[0]])
                act(T[1][:, c0:c0 + n], pm[:, 0:n], AF.Square, [pmb], [Tb[1]])
                tt("dve", T[1][:, c0:c0 + n], pq[:, 0:n], T[1][:, c0:c0 + n], ALU.subtract, [pqb, Tb[1]], [Tb[1]])
                act(T[1][:, c0:c0 + n], T[1][:, c0:c0 + n], AF.Ln, [Tb[1], dconb], [Tb[1]], bias=dc("eps_ln"))
                act(T[1][:, c0:c0 + n], T[1][:, c0:c0 + n], AF.Exp, [Tb[1]], [Tb[1]], scale=-0.5)
            for ch in range(4):
                tt("dve", T[2][:, 0:W], T[5 + ch][:, 0:W], T[0][:, 0:W], ALU.subtract, [Tb[5 + ch], Tb[0]], [Tb[2]])
                tt("dve", T[2][:, 0:W], T[2][:, 0:W], T[1][:, 0:W], ALU.mult, [Tb[2], Tb[1]], [Tb[2]])
                act(T[3][:, 0:W], T[2][:, 0:W], AF.Identity, [Tb[2], dconb], [Tb[3]], scale=dc("clgh", ch), bias=dc("clbh", ch))
                act(T[4][:, 0:W], T[3][:, 0:W], AF.Tanh, [Tb[3]], [Tb[4]])
                stt(cact[:, ch, 0:W], T[4][:, 0:W], 1.0, T[3][:, 0:W], ALU.add, ALU.mult, [Tb[4], Tb[3]], [cactb])

        S.add("pool", lambda e: e.memset(T[9][:], 1.0), writes=[Tb[9]])
        ONES = T[9]
        ONESb = Tb[9]

        def hgrn_front2(W, hg, hl, uf, uq, need_q):
            h = hg * 4 + hl
            nch = W // 64
            gs = groups_of(W)
            for g in gs:
                c0, n = g
                pf, pfb = proj_fm(uf, hl, g)
                act(T[0][:, c0:c0 + n], pf[:, 0:n], AF.Tanh, [pfb], [Tb[0]], scale=0.5)
            act(T[1][:, 0:W], T[0][:, 0:W], AF.Ln, [Tb[0], dconb], [Tb[1]], scale=dc("a0", h), bias=dc("c0", h))
            ts("dve", T[2][:, 0:W], T[0][:, 0:W], dc("na0", h), dc("a0", h), ALU.mult, ALU.add, [Tb[0], dconb], [Tb[2]])
            for c in range(nch):
                S.add("dve", lambda e, c=c: e.tensor_tensor_scan(out=T[3][:, c * 64:(c + 1) * 64], data0=ONES[:, c * 64:(c + 1) * 64],
                                                               data1=T[1][:, c * 64:(c + 1) * 64], initial=0.0, op0=ALU.mult, op1=ALU.add),
                      reads=[Tb[1], ONESb], writes=[Tb[3]])
            b3 = T[3][:, 0:W].rearrange("p (c j) -> p c j", j=64)
            t43 = T[4][:, 0:W].rearrange("p (c j) -> p c j", j=64)
            tt("dve", t43, b3[:, :, 63:64].broadcast_to([128, nch, 64]), b3, ALU.subtract, [Tb[3]], [Tb[4]])
            act(T[4][:, 0:W], T[4][:, 0:W], AF.Exp, [Tb[4]], [Tb[4]])
            tt("dve", T[5][:, 0:W], T[2][:, 0:W], T[4][:, 0:W], ALU.mult, [Tb[2], Tb[4]], [Tb[5]])
            act(dch[hl][:, 0:nch], b3[:, :, 63:64].rearrange("p c j -> p (c j)"), AF.Exp, [Tb[3]], [dchb[hl]])
            for p0 in range(0, W // 128, 4):
                ps, pb = next_ps()
                np_ = min(4, W // 128 - p0)
                for pp in range(np_):
                    p = p0 + pp
                    S.add("pe", lambda e, ps=ps, pp=pp, p=p: e.transpose(out=ps[:, pp * 128:(pp + 1) * 128], in_=T[5][:, p * 128:(p + 1) * 128], identity=cs("ident", 0, 128)),
                          reads=[Tb[5], cstb], writes=[pb], signal=(pp == np_ - 1))
                cp("act", Ktok[hl][:, p0 * 128:(p0 + np_) * 128], ps[:, 0:np_ * 128], [pb], [Ktokb[hl]])
            if need_q:
                act(dmid[hl][:, 0:nch], b3[:, :, 31:32].rearrange("p c j -> p (c j)"), AF.Exp, [Tb[3]], [dchb[hl]])
                tt("dve", t43, b3, b3[:, :, 31:32].broadcast_to([128, nch, 64]), ALU.subtract, [Tb[3]], [Tb[4]])
                act(T[6][:, 0:W], T[4][:, 0:W], AF.Exp, [Tb[4]], [Tb[6]], bias=float(np.log(0.5)))
                act(T[4][:, 0:W], T[4][:, 0:W], AF.Exp, [Tb[4]], [Tb[4]], scale=-1.0)
                tt("dve", Kb[hl][:, 0:W], T[2][:, 0:W], T[4][:, 0:W], ALU.mult, [Tb[2], Tb[4]], [Kbb[hl]])
                for g in gs:
                    c0, n = g
                    pq, pqb = proj_fm(uq, hl, g)
                    act(T[0][:, c0:c0 + n], pq[:, 0:n], AF.Tanh, [pqb], [Tb[0]], scale=0.5)
                    stt(T[0][:, c0:c0 + n], T[0][:, c0:c0 + n], 1.0, pq[:, 0:n], ALU.add, ALU.mult, [Tb[0], pqb], [Tb[0]])
                tt("dve", Qb[hl][:, 0:W], T[0][:, 0:W], T[6][:, 0:W], ALU.mult, [Tb[0], Tb[6]], [Qbb[hl]])

        def v_stage(W, ui):
            for p in range(W // 128):
                ps, pb = next_ps()
                for k in range(8):
                    r, rb = wsl(ui, k, 0, 512)
                    mm(ps[:, :], xT[:, k, p * 128:(p + 1) * 128], r, k == 0, k == 7, [xTb, rb], [pb], k == 7)
                cp(alt2(), vtok[:, p, :], ps[:, :], [pb], [vtokb])

        def state_update(hg, hl, p, c):
            h = hg * 4 + hl
            rows = slice(c * 64, (c + 1) * 64)
            pd, pdb = next_ps()
            mm(pd[:, 0:128], Ktok[hl][rows, p * 128:(p + 1) * 128], vtok[rows, p, hl * 128:(hl + 1) * 128], True, True,
               [Ktokb[hl], vtokb], [pdb], True)
            stt(Sst[:, h, :], Sst[:, h, :], dch[hl][:, 2 * p + c:2 * p + c + 1], pd[:, 0:128], ALU.mult, ALU.add,
                [Sstb[h], dchb[hl], pdb], [Sstb[h]])

        def hgrn_out_stage(W, hg, hl, po, pob, ug):
            h = hg * 4 + hl
            gs = groups_of(W)
            for gi, (c0, n) in enumerate(gs):
                ob, obb = po[hl][gi], pob[hl][gi]
                act(osq[:, c0:c0 + n], ob[:, 0:n], AF.Square, [obb], [osqb])
                pm, pmb = next_ps()
                mm(pm[:, 0:n], onesH[:], osq[:, c0:c0 + n], True, True, [miscb, osqb], [pmb], True)
                act(T[1][:, c0:c0 + n], pm[:, 0:n], AF.Ln, [pmb, dconb], [Tb[1]], bias=dc("eps_rms"))
                act(T[1][:, c0:c0 + n], T[1][:, c0:c0 + n], AF.Exp, [Tb[1]], [Tb[1]], scale=-0.5)
                tt("dve", T[2][:, c0:c0 + n], ob[:, 0:n], T[1][:, c0:c0 + n], ALU.mult, [obb, Tb[1]], [Tb[2]])
                pg, pgb = proj_fm(ug, hl, (c0, n))
                act(T[3][:, c0:c0 + n], pg[:, 0:n], AF.Tanh, [pgb], [Tb[3]], scale=0.5)
                stt(T[3][:, c0:c0 + n], T[3][:, c0:c0 + n], 1.0, pg[:, 0:n], ALU.add, ALU.mult, [Tb[3], pgb], [Tb[3]])
                stt(og[:, h, c0:c0 + n], T[2][:, c0:c0 + n], dc("gnh", h), T[3][:, c0:c0 + n], ALU.mult, ALU.mult,
                    [Tb[2], Tb[3], dconb], [ogb[h]])

        def hgrn_stage(W, full):
            gs = groups_of(W)
            for hg in range(2):
                uf = take(f"f{hg}")
                if full:
                    uq = take(f"q{hg}")
                else:
                    uq = None
                for hl in range(4):
                    hgrn_front2(W, hg, hl, uf, uq, full)
                rel()
                ui = take(f"i{hg}")
                v_stage(W, ui)
                if full:
                    ug = take(f"g{hg}")
                    for sub in range(2):
                        po = {}
                        pob = {}
                        for hl in (2 * sub, 2 * sub + 1):
                            po[hl] = []
                            pob[hl] = []
                            for _ in gs:
                                a, b_ = next_ps(hold=True)
                                po[hl].append(a)
                                pob[hl].append(b_)
                        hgrn_scan_sub(W, hg, (2 * sub, 2 * sub + 1), po, pob)
                        for hl in (2 * sub, 2 * sub + 1):
                            hgrn_out_stage(W, hg, hl, po, pob, ug)
                            for b_ in pob[hl]:
                                unhold(b_)
                    rel()
                else:
                    rel()
                    for p in range(W // 128):
                        for c in range(2):
                            for hl in range(4):
                                state_update(hg, hl, p, c)

        def hgrn_scan_sub(W, hg, hls, po, pob):
            gs = groups_of(W)
            for p in range(W // 128):
                sm_of = {}
                for hl in hls:
                    pss, pssb = next_ps()
                    mm(pss[:, 0:128], Kb[hl][:, p * 128:(p + 1) * 128], Qb[hl][:, p * 128:(p + 1) * 128], True, True,
                       [Kbb[hl], Qbb[hl]], [pssb], True)
                    si = state["smk"]
                    state["smk"] ^= 1
                    tt("dve", smk[si][:], pss[:, 0:128], maskb[:], ALU.mult, [pssb, miscb], [smkb[si]])
                    sm_of[hl] = si
                for c in range(2):
                    for hl in hls:
                        h = hg * 4 + hl
                        si = sm_of[hl]
                        col = p * 128 + c * 64
                        gi = 0
                        for j, (c0, n) in enumerate(gs):
                            if c0 <= col < c0 + n:
                                gi = j
                        c0 = gs[gi][0]
                        ob, obb = po[hl][gi], pob[hl][gi]
                        rows = slice(c * 64, (c + 1) * 64)
                        act(Sbf[:, h, :], Sst[:, h, :], AF.Identity, [Sstb[h], dchb[hl]], [Sbfb[h]], scale=dmid[hl][:, 2 * p + c:2 * p + c + 1])
                        mm(ob[:, col - c0:col - c0 + 64], vtok[rows, p, hl * 128:(hl + 1) * 128], smk[si][rows, c * 64:(c + 1) * 64],
                           True, False, [vtokb, smkb[si]], [obb], False)
                        mm(ob[:, col - c0:col - c0 + 64], Sbf[:, h, :], Qb[hl][:, col:col + 64], False, True,
                           [Sbfb[h], Qbb[hl]], [obb], True)
                        state_update(hg, hl, p, c)

        def merge_stage(W):
            gs = groups_of(W)
            for half in range(2):
                um0 = take(f"m0_{half}")
                uco = take(f"co{half}")
                for j in range(4):
                    for g in gs:
                        c0, n = g
                        pm0, pm0b = proj_fm(um0, j, g)
                        act(T[4][:, c0:c0 + n], pm0[:, 0:n], AF.Tanh, [pm0b], [Tb[4]], scale=0.5)
                        pyc, pycb = proj_fm(uco, j, g, kn=4, src=cact, srcb=[cactb])
                        stt(T[j][:, c0:c0 + n], T[4][:, c0:c0 + n], 1.0, pyc[:, 0:n], ALU.add, ALU.mult, [Tb[4], pycb], [Tb[j]])
                rel()
                um1 = take(f"m1_{half}")
                uho = take(f"ho{half}")
                for j in range(4):
                    jj = half * 4 + j
                    for g in gs:
                        c0, n = g
                        pm1, pm1b = proj_fm(um1, j, g)
                        act(T[4][:, c0:c0 + n], pm1[:, 0:n], AF.Tanh, [pm1b], [Tb[4]], scale=0.5)
                        pyh, pyhb = proj_fm(uho, j, g, kn=8, src=og, srcb=ogb)
                        stt(T[5][:, c0:c0 + n], T[4][:, c0:c0 + n], 1.0, pyh[:, 0:n], ALU.add, ALU.mult, [Tb[4], pyhb], [Tb[5]])
                        tt("dve", mixT[:, jj, c0:c0 + n], T[j][:, c0:c0 + n], T[5][:, c0:c0 + n], ALU.add, [Tb[j], Tb[5]], [mixTb])
                rel()

        def layer_norm_tok(buf, bufb, gi, bi):
            for hh in range(2):
                S.add("dve", lambda e, hh=hh: e.bn_stats(out=bst[:, hh, :], in_=buf[:, hh * 512:(hh + 1) * 512]), reads=[bufb], writes=[bstb])
            S.add("dve", lambda e: e.bn_aggr(out=mv[:], in_=bst[:].rearrange("p a b -> p (a b)")), reads=[bstb], writes=[mvb])
            ts("pool", mv[:, 1:2], mv[:, 1:2], LN_EPS, None, ALU.add, None, [mvb], [mvb])
            tt("pool", mv[:, 1:2], mv[:, 1:2], dc("negh"), ALU.pow, [mvb, dconb], [mvb])
            ts("dve", buf[:], buf[:], mv[:, 0:1], mv[:, 1:2], ALU.subtract, ALU.mult, [bufb, mvb], [bufb])
            tt("dve", buf[:], buf[:], lnt[:, gi, :], ALU.mult, [bufb, lntb], [bufb])
            tt("dve", buf[:], buf[:], lnt[:, bi, :], ALU.add, [bufb, lntb], [bufb])

        def wout_ln1_stage(row0, W):
            uw = [take("wo0"), take("wo1")]
            npair = W // 128
            halo = (W == 640)
            for p in range(npair):
                if halo and p == 0:
                    dst, dstb = x1h, x1hb
                else:
                    mp = p - (1 if halo else 0)
                    dst, dstb = x1tok[mp], x1tokb[mp]
                sem = xtoksem[0] if (halo and p == 0) else x1sem[mp]
                S.add("pool", lambda e, dst=dst, r=row0 + p * 128: e.dma_start(out=dst[:], in_=xc[r:r + 128, :]),
                      writes=[dstb], dsem=sem)
                ts("dve", dst[:], dst[:], ALPHA, None, ALU.mult, None, [dstb], [dstb])
                for ch2 in range(2):
                    ps, pb = next_ps()
                    for k in range(8):
                        r, rb = wsl(uw[ch2], k, 0, 512)
                        mm(ps[:, :], mixT[:, k, p * 128:(p + 1) * 128], r, k == 0, k == 7, [mixTb, rb], [pb], k == 7)
                    stt(dst[:, ch2 * 512:(ch2 + 1) * 512], ps[:, :], 0.5, dst[:, ch2 * 512:(ch2 + 1) * 512], ALU.mult, ALU.add, [pb, dstb], [dstb])
                layer_norm_tok(dst, dstb, 0, 1)
            rel()
            for p in range(npair):
                if halo and p == 0:
                    src, srcb = x1h, x1hb
                else:
                    mp = p - (1 if halo else 0)
                    src, srcb = x1tok[mp], x1tokb[mp]
                for kb in range(2):
                    ps, pb = next_ps()
                    for kk in range(4):
                        k = kb * 4 + kk
                        S.add("pe", lambda e, ps=ps, kk=kk, k=k, src=src: e.transpose(out=ps[:, kk * 128:(kk + 1) * 128], in_=src[:, k * 128:(k + 1) * 128], identity=cs("ident", 0, 128)),
                              reads=[srcb, cstb], writes=[pb], signal=(kk == 3))
                    cp(alt2(), mixT[:, kb * 4:kb * 4 + 4, p * 128:(p + 1) * 128], ps[:, :].rearrange("p (k c) -> p k c", c=128), [pb], [mixTb])

        def ffn_in_stage(W):
            halo = (W == 640)
            m0 = 128 if halo else 0
            gs = groups_of(W)
            for j in range(6):
                uu = take(f"u{j}")
                ugv = take(f"gv{j}")
                ncl = 4 if j < 5 else 2
                for cl in range(ncl):
                    hc = j * 4 + cl
                    ui = hc % 2
                    ub, ubb = ubuf[ui], ubufb[ui]
                    cp("pool", ub[:, 0:2], uhist[:, hc, :], [uhistb], [ubb])
                    for g in gs:
                        c0, n = g
                        pu, pub = proj_fm(uu, cl, g, src=mixT, srcb=[mixTb])
                        cp("act", ub[:, 2 + c0:2 + c0 + n], pu[:, 0:n], [pub], [ubb])
                    if halo:
                        ts("pool", ub[:, 2:130], ub[:, 2:130], cs("flag", 0, 1), None, ALU.mult, None, [ubb, cstb], [ubb])
                    cp("pool", uhist[:, hc, :], ub[:, W:W + 2], [ubb], [uhistb])
                    pc, pcb = next_ps()
                    for k in range(3):
                        di = state["dg"]
                        state["dg"] = (di + 1) % NDG
                        ts("dve", dg[di][:], cs("ident", 0, 128), cs("ffw", hc * 3 + k, 1), None, ALU.mult, None, [cstb], [dgb[di]])
                        mm(pc[:, :], dg[di][:], ub[:, m0 + k:m0 + k + 512], k == 0, k == 2, [dgb[di], ubb], [pcb], k == 2)
                    act(T[0][:, 0:512], pc[:, :], AF.Square, [pcb, cstb], [Tb[0]], bias=cs("ffb", hc, 1))
                    act(T[1][:, 0:512], pc[:, :], AF.Identity, [pcb, cstb], [Tb[1]], bias=cs("ffb", hc, 1))
                    ts("dve", T[0][:, 0:512], T[0][:, 0:512], GC2, GC1, ALU.mult, ALU.add, [Tb[0]], [Tb[0]])
                    tt("dve", T[0][:, 0:512], T[0][:, 0:512], T[1][:, 0:512], ALU.mult, [Tb[0], Tb[1]], [Tb[0]])
                    act(T[0][:, 0:512], T[0][:, 0:512], AF.Tanh, [Tb[0]], [Tb[0]])
                    stt(T[0][:, 0:512], T[0][:, 0:512], 1.0, T[1][:, 0:512], ALU.add, ALU.mult, [Tb[0], Tb[1]], [Tb[0]])
                    pg, pgb = proj_fm(ugv, cl, (m0, 512), src=mixT, srcb=[mixTb])
                    stt(hT[:, hc, :], T[0][:, 0:512], 0.5, pg[:, :], ALU.mult, ALU.mult, [Tb[0], pgb], [hTb[hc]])
                rel()

        def ffn_out_stage(orow0):
            for ch2 in range(2):
                ufo = take(f"fo{ch2}")
                pss = [next_ps(hold=True) for _ in range(4)]
                for hc in range(NFC):
                    r, rb = wsl(ufo, hc, 0, 512)
                    for p in range(4):
                        ps, pb = pss[p]
                        mm(ps[:, :], hT[:, hc, p * 128:(p + 1) * 128], r, hc == 0, hc == NFC - 1, [hTb[hc], rb], [pb], hc == NFC - 1)
                for p in range(4):
                    ps, pb = pss[p]
                    stt(x1tok[p][:, ch2 * 512:(ch2 + 1) * 512], x1tok[p][:, ch2 * 512:(ch2 + 1) * 512], ALPHA, ps[:, :], ALU.mult, ALU.add,
                        [x1tokb[p], pb], [x1tokb[p]])
                    unhold(pb)
                rel()
            for p in range(4):
                layer_norm_tok(x1tok[p], x1tokb[p], 2, 3)
                S.add("pool", lambda e, p=p, r=orow0 + p * 128: e.dma_start(out=outd[r:r + 128, :], in_=x1tok[p][:]),
                      reads=[x1tokb[p]], writes=[outb[p]], dsem=outsem[p])

        for kind, row0, W, flag in tiles:
            load_x(row0, W)
            if kind == "pre":
                if flag:
                    glu_stage(W)
                    cbuf_carry(W)
                hgrn_stage(W, False)
            else:
                glu_stage(W)
                conv_stage(W)
                cbuf_carry(W)
                hgrn_stage(W, True)
                merge_stage(W)
                wout_ln1_stage(row0, W)
                ffn_in_stage(W)
                orow0 = (row0 + 128 - HALF) if W == 640 else (row0 - HALF)
                ffn_out_stage(orow0)
        S.final_waits("pool", outb)
        assert wpos["p"] == len(WS.seq)
        S.emit(nc)
    return nc


def _pack_weights(mats):
    wp = np.zeros((NUNITS, 128, 4, 512), np.float32)
    for name in UNAMES:
        for i, (mat, r0, nr, c0, ncol) in enumerate(UT[name]):
            blk = mats[mat][r0:r0 + nr, c0:c0 + ncol]
            nk = nr // 128
            wp[UBASE[name] + i, :, :nk, :ncol] = blk.reshape(nk, 128, ncol).transpose(1, 0, 2)
    return wp.reshape(NUNITS, 128, UC)


def _pack_consts(inp, flag):
    c = np.zeros((128, NCONST), np.float32)

    def put(name, arr):
        o, w = CO[name]
        assert arr.shape == (128, w), (name, arr.shape)
        c[:, o:o + w] = arr
    cw = inp["w_conv_dw"][0]
    put("convw", cw.T.reshape(4, 128, CONV_K).transpose(1, 0, 2).reshape(128, 4 * CONV_K))
    put("convb", inp["b_conv_dw"][0].reshape(4, 128).T)
    put("clg", inp["conv_ln_g"][0].reshape(4, 128).T)
    put("clb", inp["conv_ln_b"][0].reshape(4, 128).T)
    put("l0", inp["hgrn_lb_logits"][0].reshape(8, 128).T)
    put("l1", inp["hgrn_lb_logits"][1].reshape(8, 128).T)
    put("gn", inp["hgrn_norm_g"][0].reshape(8, 128).T)
    fw = inp["w_ffn_dw"][0]
    put("ffw", fw.T.reshape(NFC, 128, 3).transpose(1, 0, 2).reshape(128, NFC * 3))
    put("ffb", inp["b_ffn_dw"][0].reshape(NFC, 128).T)
    put("flag", np.full((128, 1), flag, np.float32))
    put("ident", np.eye(128, dtype=np.float32))
    s = np.arange(128)[:, None]
    t = np.arange(128)[None, :]
    put("mask", ((s <= t) & ((s // 64) == (t // 64))).astype(np.float32))
    return c


_CACHE = {}


def kernel(x, w_in, w_conv_dw, b_conv_dw, conv_ln_g, conv_ln_b, w_conv_out, hgrn_lb_logits, hgrn_norm_g,
           w_hgrn_out, w_out, ln1_g, ln1_b, w_ffn_in, w_ffn_dw, b_ffn_dw, w_ffn_out, ln2_g, ln2_b):
    inp = dict(w_conv_dw=np.asarray(w_conv_dw, np.float32), b_conv_dw=np.asarray(b_conv_dw, np.float32),
               conv_ln_g=np.asarray(conv_ln_g, np.float32), conv_ln_b=np.asarray(conv_ln_b, np.float32),
               hgrn_lb_logits=np.asarray(hgrn_lb_logits, np.float32), hgrn_norm_g=np.asarray(hgrn_norm_g, np.float32),
               w_ffn_dw=np.asarray(w_ffn_dw, np.float32), b_ffn_dw=np.asarray(b_ffn_dw, np.float32))
    mats = {"w_in": np.asarray(w_in, np.float32)[0], "w_conv_out": np.asarray(w_conv_out, np.float32)[0],
            "w_hgrn_out": np.asarray(w_hgrn_out, np.float32)[0], "w_out": np.asarray(w_out, np.float32)[0],
            "w_ffn_in": np.asarray(w_ffn_in, np.float32)[0], "w_ffn_out": np.asarray(w_ffn_out, np.float32)[0]}
    x = np.asarray(x, np.float32)
    wp = _pack_weights(mats)
    lnp = np.stack([np.broadcast_to(np.asarray(a, np.float32)[0][None, :], (128, D)) for a in (ln1_g, ln1_b, ln2_g, ln2_b)], axis=1)
    lnp = np.ascontiguousarray(lnp)
    if "nc" not in _CACHE:
        _CACHE["nc"] = build_program()
    nc = _CACHE["nc"]
    in_maps = []
    for core in range(8):
        b, half = core // 2, core % 2
        if half == 0:
            xcore = np.concatenate([np.zeros((HALF, D), np.float32), x[b, :HALF]], axis=0)
        else:
            xcore = x[b]
        in_maps.append({"xc": np.ascontiguousarray(xcore), "wpack": wp, "cpack": _pack_consts(inp, float(half)), "lnpack": lnp})
    res = run_bass_kernel_spmd(nc, in_maps, core_ids=list(range(8)))
    out = np.zeros((NB, SEQ, D), np.float32)
    for core in range(8):
        b, half = core // 2, core % 2
        out[b, half * HALF:(half + 1) * HALF] = res.results[core]["out"]
    return out
```
